# Optimizing a Trainium2 kernel written in Bass

```python
import math
import jax
import jax.numpy as jnp
from jax import lax
import numpy as np

D_MODEL = 1024
BATCH = 4
SEQ = 4096
DEPTH = 1
DEC_BATCH = 128
DEC_SEQ = 8
PAST_LEN = 2048
PAGE_SIZE = 128

ATT_GROUPS = ((128, 1), (512, 4), (2048, 16))
N_ATT_GROUPS = len(ATT_GROUPS)
HEAD_DIM = 64
HEADS_PER_GROUP = D_MODEL // 128
ATT_WIDTH = HEADS_PER_GROUP * HEAD_DIM
QKV_WIDTH = N_ATT_GROUPS * ATT_WIDTH
ATT_BLOCK = 128
ALIBI_MAX_EXP = 8.0

SSM_EXPAND = 2
D_INNER = SSM_EXPAND * D_MODEL
SSM_HEAD_DIM = 64
SSM_HEADS = D_INNER // SSM_HEAD_DIM
SSM_GROUPS = 4
D_STATE = 128
CONV_WIDTH = 4
CONV_DIM = D_INNER + 2 * SSM_GROUPS * D_STATE
SSD_CHUNK = 128

IN_SIZES = (QKV_WIDTH, QKV_WIDTH, QKV_WIDTH, ATT_WIDTH, D_INNER, CONV_DIM, SSM_HEADS, D_MODEL, D_MODEL)
IN_WIDTH = sum(IN_SIZES)
NORM_EPS = 1e-6
NEG_INF = -1e30

kernel_name = "griffin_dilated_attn_mamba2_step"


def _in_offsets():
    offs, acc = [], 0
    for s in IN_SIZES[:-1]:
        acc += s
        offs.append(acc)
    return offs


def _alibi_slopes():
    n = N_ATT_GROUPS * HEADS_PER_GROUP
    m = 2.0 ** (-ALIBI_MAX_EXP * np.arange(1, n + 1) / n)
    return jnp.asarray(m.reshape(N_ATT_GROUPS, HEADS_PER_GROUP), dtype=jnp.float32)


def _rmsnorm(x, g):
    xf = x.astype(jnp.float32)
    xf = xf * lax.rsqrt(jnp.mean(jnp.square(xf), axis=-1, keepdims=True) + NORM_EPS)
    return (xf * g.astype(jnp.float32)).astype(x.dtype)


def _front(x, c, norm_g, w_ada, b_ada, w_in):
    mod = jax.nn.silu(c) @ w_ada + b_ada
    shift, scale, gate = jnp.split(mod, 3, axis=-1)
    h = _rmsnorm(x, norm_g) * (1.0 + scale[:, None, :]) + shift[:, None, :]
    parts = jnp.split(h @ w_in, _in_offsets(), axis=-1)
    return parts, gate


def _split_heads(t):
    b, L, _ = t.shape
    return t.reshape(b, L, N_ATT_GROUPS, HEADS_PER_GROUP, HEAD_DIM)


def _dilated_attn_prompt(q, k, v, slopes, window, dil):
    b, T, H, hd = q.shape
    span = window // dil
    n_sub = -(-T // dil)
    n_pad = -(-n_sub // ATT_BLOCK) * ATT_BLOCK
    nb = n_pad // ATT_BLOCK
    pad = n_pad * dil - T

    def to_blocks(t):
        t = jnp.pad(t, ((0, 0), (0, pad), (0, 0), (0, 0)))
        return t.reshape(b, nb, ATT_BLOCK, dil, H, hd)

    def with_prev(t):
        prev = jnp.concatenate([jnp.zeros_like(t[:, :1]), t[:, :-1]], axis=1)
        return jnp.concatenate([prev, t], axis=2)

    qb = to_blocks(q)
    kk = with_prev(to_blocks(k))
    vv = with_prev(to_blocks(v))
    s = jnp.einsum('bnqrhd,bnkrhd->bnrhqk', qb, kk,
                   preferred_element_type=jnp.float32) * (hd ** -0.5)
    qi = jnp.arange(ATT_BLOCK)[:, None]
    kj = jnp.arange(2 * ATT_BLOCK)[None, :]
    delta = qi + ATT_BLOCK - kj
    blk = jnp.arange(nb)[:, None, None]
    valid = (delta >= 0) & (delta <= span) & ((blk > 0) | (kj >= ATT_BLOCK))
    alibi = -slopes[:, None, None] * (delta * dil).astype(jnp.float32)
    s = jnp.where(valid[None, :, None, None], s + alibi[None, None, None], NEG_INF)
    lse = jax.nn.logsumexp(s, axis=-1)
    p = jnp.exp(s - lse[..., None]).astype(v.dtype)
    o = jnp.einsum('bnrhqk,bnkrhd->bnqrhd', p, vv)
    o = o.reshape(b, n_pad * dil, H, hd)[:, :T]
    lse = lse.transpose(0, 1, 4, 2, 3).reshape(b, n_pad * dil, H)[:, :T]
    return o, lse


def _dilated_attn_sample(q, k_all, v_all, slopes, window, dil):
    b, S, H, hd = q.shape
    Lc = k_all.shape[1] - S
    span = window // dil
    j = jnp.arange(span + 1)
    idx = Lc + jnp.arange(S)[:, None] - dil * j[None, :]
    valid = idx >= 0
    idx = jnp.maximum(idx, 0)
    kg = jnp.take(k_all, idx, axis=1)
    vg = jnp.take(v_all, idx, axis=1)
    s = jnp.einsum('bshd,bsjhd->bhsj', q, kg,
                   preferred_element_type=jnp.float32) * (hd ** -0.5)
    s = s - slopes[:, None, None] * (dil * j).astype(jnp.float32)[None, None, :]
    s = jnp.where(valid[None, None], s, NEG_INF)
    lse = jax.nn.logsumexp(s, axis=-1)
    p = jnp.exp(s - lse[..., None]).astype(v_all.dtype)
    o = jnp.einsum('bhsj,bsjhd->bshd', p, vg)
    return o, lse.transpose(0, 2, 1)


def _combine_groups(outs, lses):
    w = jax.nn.softmax(jnp.stack(lses, 0), axis=0)
    o = jnp.einsum('gblh,gblhd->blhd', w.astype(outs[0].dtype), jnp.stack(outs, 0))
    return o.reshape(o.shape[0], o.shape[1], ATT_WIDTH)


def _causal_dwconv(xpad, w, bias):
    out = lax.conv_general_dilated(xpad, w[:, None, :], window_strides=(1,), padding='VALID',
                                   dimension_numbers=('NWC', 'WIO', 'NWC'),
                                   feature_group_count=xpad.shape[-1])
    return out + bias


def _ssd(x, dt, a, bm, cm, init_state, chunk):
    f32 = jnp.float32
    b, L, h, p = x.shape
    G, n = bm.shape[2], bm.shape[3]
    E = h // G
    nc = L // chunk
    xdt = (x.astype(f32) * dt[..., None]).reshape(b, nc, chunk, G, E, p)
    da = (dt * a).reshape(b, nc, chunk, G, E).transpose(0, 3, 4, 1, 2)
    bm = bm.astype(f32).reshape(b, nc, chunk, G, n)
    cm = cm.astype(f32).reshape(b, nc, chunk, G, n)
    da_cs = jnp.cumsum(da, axis=-1)
    causal = jnp.tril(jnp.ones((chunk, chunk), dtype=bool))
    seg = da_cs[..., :, None] - da_cs[..., None, :]
    decay = jnp.exp(jnp.where(causal, seg, -jnp.inf))
    cb = jnp.einsum('bclgn,bcsgn->bcgls', cm, bm)
    y_diag = jnp.einsum('bcgls,bgecls,bcsgep->bclgep', cb, decay, xdt)
    decay_to_end = jnp.exp(da_cs[..., -1:] - da_cs)
    states = jnp.einsum('bclgn,bgecl,bclgep->bcgepn', bm, decay_to_end, xdt)
    chunk_tot = da_cs[..., -1]

    def step(carry, inp):
        tot, st = inp
        return jnp.exp(tot)[..., None, None] * carry + st, carry

    init = init_state.astype(f32).reshape(b, G, E, p, n)
    final, prev = lax.scan(step, init, (jnp.moveaxis(chunk_tot, -1, 0), jnp.moveaxis(states, 1, 0)))
    prev = jnp.moveaxis(prev, 0, 1)
    y_off = jnp.einsum('bclgn,bcgepn,bgecl->bclgep', cm, prev, jnp.exp(da_cs))
    y = (y_diag + y_off).reshape(b, L, h, p)
    return y, final.reshape(b, h, p, n)


def _gated_rmsnorm(y, z, g):
    b, L, _ = y.shape
    u = (y * jax.nn.silu(z.astype(jnp.float32))).reshape(b, L, SSM_GROUPS, D_INNER // SSM_GROUPS)
    u = u * lax.rsqrt(jnp.mean(jnp.square(u), axis=-1, keepdims=True) + NORM_EPS)
    return u.reshape(b, L, D_INNER) * g.astype(jnp.float32)


def _ssm_branch(xpad, z, dt_raw, init_state, chunk, conv_w, conv_b, dt_bias, a_log, d_skip, ssm_norm_g):
    b, L = z.shape[:2]
    xbc = jax.nn.silu(_causal_dwconv(xpad, conv_w, conv_b))
    xs, bm, cm = jnp.split(xbc, [D_INNER, D_INNER + SSM_GROUPS * D_STATE], axis=-1)
    xs = xs.reshape(b, L, SSM_HEADS, SSM_HEAD_DIM)
    bm = bm.reshape(b, L, SSM_GROUPS, D_STATE)
    cm = cm.reshape(b, L, SSM_GROUPS, D_STATE)
    dt = jax.nn.softplus(dt_raw.astype(jnp.float32) + dt_bias.astype(jnp.float32))
    a = -jnp.exp(a_log.astype(jnp.float32))
    y, final = _ssd(xs, dt, a, bm, cm, init_state, chunk)
    y = y + d_skip.astype(jnp.float32)[:, None] * xs.astype(jnp.float32)
    y = _gated_rmsnorm(y.reshape(b, L, D_INNER), z, ssm_norm_g).astype(z.dtype)
    return y, final.astype(init_state.dtype)


def _back(x, gate, att, g_att, y_ssm, g_a, g_b, w_att_branch, w_ssm_branch, w_out):
    a_out = (att * jax.nn.silu(g_att)) @ w_att_branch
    m_out = y_ssm @ w_ssm_branch
    merged = jax.nn.sigmoid(g_a) * a_out + jax.nn.sigmoid(g_b) * m_out
    return x + gate[:, None, :] * (merged @ w_out)


def _prompt_layer(x, c, norm_g, w_ada, b_ada, w_in, conv_w, conv_b, dt_bias, a_log, d_skip,
                  ssm_norm_g, w_att_branch, w_ssm_branch, w_out):
    b, T, _ = x.shape
    (q, k, v, g_att, z, xbc, dt_raw, g_a, g_b), gate = _front(x, c, norm_g, w_ada, b_ada, w_in)
    q, k, v = _split_heads(q), _split_heads(k), _split_heads(v)
    slopes = _alibi_slopes()
    outs, lses, kv_states = [], [], []
    for g, (win, dil) in enumerate(ATT_GROUPS):
        o, lse = _dilated_attn_prompt(q[:, :, g], k[:, :, g], v[:, :, g], slopes[g], win, dil)
        outs.append(o)
        lses.append(lse)
        keep = min(win, T)
        kv_states.append(jnp.stack([k[:, T - keep:, g], v[:, T - keep:, g]], axis=2))
    att = _combine_groups(outs, lses)
    xpad = jnp.pad(xbc, ((0, 0), (CONV_WIDTH - 1, 0), (0, 0)))
    init = jnp.zeros((b, SSM_HEADS, SSM_HEAD_DIM, D_STATE), x.dtype)
    y_ssm, ssm_state = _ssm_branch(xpad, z, dt_raw, init, math.gcd(SSD_CHUNK, T), conv_w, conv_b,
                                   dt_bias, a_log, d_skip, ssm_norm_g)
    x = _back(x, gate, att, g_att, y_ssm, g_a, g_b, w_att_branch, w_ssm_branch, w_out)
    return x, kv_states, ssm_state, xpad[:, -(CONV_WIDTH - 1):]


def _sample_layer(x, c, kv0, kv1, kv2, ssm_state, conv_state, norm_g, w_ada, b_ada, w_in, conv_w,
                  conv_b, dt_bias, a_log, d_skip, ssm_norm_g, w_att_branch, w_ssm_branch, w_out):
    b, S, _ = x.shape
    (q, k, v, g_att, z, xbc, dt_raw, g_a, g_b), gate = _front(x, c, norm_g, w_ada, b_ada, w_in)
    q, k, v = _split_heads(q), _split_heads(k), _split_heads(v)
    slopes = _alibi_slopes()
    caches = (kv0, kv1, kv2)
    outs, lses, kv_states = [], [], []
    for g, (win, dil) in enumerate(ATT_GROUPS):
        kv_all = jnp.concatenate([caches[g], jnp.stack([k[:, :, g], v[:, :, g]], axis=2)], axis=1)
        o, lse = _dilated_attn_sample(q[:, :, g], kv_all[:, :, 0], kv_all[:, :, 1], slopes[g], win, dil)
        outs.append(o)
        lses.append(lse)
        kv_states.append(kv_all[:, -caches[g].shape[1]:])
    att = _combine_groups(outs, lses)
    xpad = jnp.concatenate([conv_state, xbc], axis=1)
    y_ssm, new_ssm = _ssm_branch(xpad, z, dt_raw, ssm_state, S, conv_w, conv_b,
                                 dt_bias, a_log, d_skip, ssm_norm_g)
    x = _back(x, gate, att, g_att, y_ssm, g_a, g_b, w_att_branch, w_ssm_branch, w_out)
    return x, kv_states, new_ssm, xpad[:, -(CONV_WIDTH - 1):]


def setup_inputs(seed: int = 0) -> dict:
    key = jax.random.key(seed)
    ks = jax.random.split(key, 24)
    f32 = jnp.float32

    def nrm(k, shape, scale=1.0):
        return jax.random.normal(k, shape, f32) * scale

    def kv_shape(win):
        return (DEPTH, DEC_BATCH, min(win, PAST_LEN), 2, HEADS_PER_GROUP, HEAD_DIM)

    dt0 = jnp.exp(jax.random.uniform(ks[13], (DEPTH, SSM_HEADS), f32, math.log(1e-3), math.log(1e-1)))
    return {
        "x_prompt": nrm(ks[0], (BATCH, SEQ, D_MODEL)),
        "x_sample": nrm(ks[1], (DEC_BATCH, DEC_SEQ, D_MODEL)),
        "c_prompt": nrm(ks[2], (BATCH, D_MODEL)),
        "c_sample": nrm(ks[3], (DEC_BATCH, D_MODEL)),
        "cache_kv_w128": nrm(ks[4], kv_shape(ATT_GROUPS[0][0])),
        "cache_kv_w512": nrm(ks[5], kv_shape(ATT_GROUPS[1][0])),
        "cache_kv_w2048": nrm(ks[6], kv_shape(ATT_GROUPS[2][0])),
        "state_ssm": nrm(ks[7], (DEPTH, DEC_BATCH, SSM_HEADS, SSM_HEAD_DIM, D_STATE), 0.1),
        "state_conv": nrm(ks[8], (DEPTH, DEC_BATCH, CONV_WIDTH - 1, CONV_DIM)),
        "norm_g": 1.0 + nrm(ks[9], (DEPTH, D_MODEL), 0.02),
        "w_ada": nrm(ks[10], (DEPTH, D_MODEL, 3 * D_MODEL), 0.5 * D_MODEL ** -0.5),
        "b_ada": nrm(ks[11], (DEPTH, 3 * D_MODEL), 0.02),
        "w_in": nrm(ks[12], (DEPTH, D_MODEL, IN_WIDTH), D_MODEL ** -0.5),
        "conv_w": nrm(ks[14], (DEPTH, CONV_WIDTH, CONV_DIM), CONV_WIDTH ** -0.5),
        "conv_b": nrm(ks[15], (DEPTH, CONV_DIM), 0.02),
        "dt_bias": dt0 + jnp.log(-jnp.expm1(-dt0)),
        "a_log": jnp.log(jax.random.uniform(ks[16], (DEPTH, SSM_HEADS), f32, 1.0, 16.0)),
        "d_skip": 1.0 + nrm(ks[17], (DEPTH, SSM_HEADS), 0.1),
        "ssm_norm_g": 1.0 + nrm(ks[18], (DEPTH, D_INNER), 0.02),
        "w_att_branch": nrm(ks[19], (DEPTH, ATT_WIDTH, D_MODEL), ATT_WIDTH ** -0.5),
        "w_ssm_branch": nrm(ks[20], (DEPTH, D_INNER, D_MODEL), D_INNER ** -0.5),
        "w_out": nrm(ks[21], (DEPTH, D_MODEL, D_MODEL), D_MODEL ** -0.5),
        "final_norm_g": 1.0 + nrm(ks[22], (D_MODEL,), 0.02),
    }


def reference(x_prompt, x_sample, c_prompt, c_sample, cache_kv_w128, cache_kv_w512, cache_kv_w2048,
              state_ssm, state_conv, norm_g, w_ada, b_ada, w_in, conv_w, conv_b, dt_bias, a_log,
              d_skip, ssm_norm_g, w_att_branch, w_ssm_branch, w_out, final_norm_g):
    xp, xs = x_prompt, x_sample
    kvp = ([], [], [])
    kvs = ([], [], [])
    ssm_p, conv_p, ssm_s, conv_s = [], [], [], []
    for l in range(DEPTH):
        xp, kv_new_p, sp, cp = _prompt_layer(
            xp, c_prompt, norm_g[l], w_ada[l], b_ada[l], w_in[l], conv_w[l], conv_b[l], dt_bias[l],
            a_log[l], d_skip[l], ssm_norm_g[l], w_att_branch[l], w_ssm_branch[l], w_out[l])
        xs, kv_new_s, ss, cs = _sample_layer(
            xs, c_sample, cache_kv_w128[l], cache_kv_w512[l], cache_kv_w2048[l], state_ssm[l],
            state_conv[l], norm_g[l], w_ada[l], b_ada[l], w_in[l], conv_w[l], conv_b[l], dt_bias[l],
            a_log[l], d_skip[l], ssm_norm_g[l], w_att_branch[l], w_ssm_branch[l], w_out[l])
        for g in range(N_ATT_GROUPS):
            kvp[g].append(kv_new_p[g])
            kvs[g].append(kv_new_s[g])
        ssm_p.append(sp)
        conv_p.append(cp)
        ssm_s.append(ss)
        conv_s.append(cs)
    y_prompt = _rmsnorm(xp, final_norm_g)
    y_sample = _rmsnorm(xs, final_norm_g)
    return (y_prompt, y_sample,
            jnp.stack(kvp[0]), jnp.stack(kvp[1]), jnp.stack(kvp[2]), jnp.stack(ssm_p), jnp.stack(conv_p),
            jnp.stack(kvs[0]), jnp.stack(kvs[1]), jnp.stack(kvs[2]), jnp.stack(ssm_s), jnp.stack(conv_s))
```

```python
import contextlib
import numpy as np
import concourse.bass as bass
import concourse.mybir as mybir
from concourse.bass_utils import run_bass_kernel_spmd

F32 = mybir.dt.float32
BF16 = mybir.dt.bfloat16
AF = mybir.ActivationFunctionType
ALU = mybir.AluOpType
AX = mybir.AxisListType

COMPUTE = ("pe", "act", "dve", "pool")
import os as _os
STAGE = int(_os.environ.get("KSTAGE", "99"))
KSUB = int(_os.environ.get("KSUB", "99"))
KSLOTS = int(_os.environ.get("KSLOTS", "4"))
KBLK = int(_os.environ.get("KBLK", "99"))

D = 1024
SEQ = 4096
HALF = 2048
NB_S = 16
IN_W = 12320
OFF_Q, OFF_K, OFF_V, OFF_GATT, OFF_Z, OFF_XBC, OFF_DT, OFF_GA, OFF_GB = 0, 1536, 3072, 4608, 5120, 7168, 10240, 10272, 11296
WINS = (128, 512, 2048)
DILS = (1, 4, 16)
EPS = 1e-6
NEG = -1e30


class Buf:
    __slots__ = ("name", "writer", "readers", "excl")

    def __init__(self, name="", excl=False):
        self.name = name
        self.writer = None
        self.readers = []
        self.excl = excl


class FW:
    def __init__(self, nc, n_dma_sems=48):
        self.nc = nc
        self.streams = {k: [] for k in ("pe", "act", "dve", "pool", "sp")}
        self.sem = {k: nc.alloc_semaphore("sem_" + k) for k in COMPUTE}
        self.count = {k: 0 for k in COMPUTE}
        self.waited = {k: {} for k in self.streams}
        self.dma_sems = [nc.alloc_semaphore("dsem%d" % i) for i in range(n_dma_sems)]
        self.dma_val = [0] * n_dma_sems
        self.dma_pool = {"sp": list(range(0, 28)), "pool": list(range(28, n_dma_sems)), "act": []}
        self.dma_rr = {"sp": 0, "pool": 0}
        self.bg_sem = nc.alloc_semaphore("bgsem")
        self.bg_val = 0
        self.semobj = {k: self.sem[k] for k in COMPUTE}
        for i, s in enumerate(self.dma_sems):
            self.semobj[("d", i)] = s
        self.semobj["bg"] = self.bg_sem

    def _need(self, reads, writes):
        need = {}

        def add(kv):
            if kv is None:
                return
            k, v = kv
            if need.get(k, -1) < v:
                need[k] = v
        for b in reads:
            add(b.writer)
            if b.excl:
                for r in b.readers:
                    add(r)
        for b in writes:
            add(b.writer)
            for r in b.readers:
                add(r)
        return need

    def _emit_waits(self, eng, need, skip_self=False):
        waits = []
        w = self.waited[eng]
        for k, v in need.items():
            if skip_self and k == eng:
                continue
            if w.get(k, -1) >= v:
                continue
            w[k] = v
            waits.append((self.semobj[k], v))
        return waits

    def _update(self, reads, writes, tag):
        for b in reads:
            b.readers.append(tag)
            if len(b.readers) > 64:
                best = {}
                for k, v in b.readers:
                    if best.get(k, -1) < v:
                        best[k] = v
                b.readers = list(best.items())
        for b in writes:
            b.writer = tag
            b.readers = []

    def op(self, eng, fn, reads=(), writes=(), inc=True):
        need = self._need(reads, writes)
        waits = self._emit_waits(eng, need, skip_self=(eng == "pe"))
        if inc:
            self.count[eng] += 1
            tag = (eng, self.count[eng])
        else:
            tag = (eng, self.count[eng] + 1)
        self.streams[eng].append((waits, fn, (self.sem[eng], 1) if inc else None))
        self._update(reads, writes, tag)

    def dma(self, fn, reads=(), writes=(), queue="sp"):
        need = self._need(reads, writes)
        pool = self.dma_pool[queue]
        i = pool[self.dma_rr[queue] % len(pool)]
        self.dma_rr[queue] += 1
        key = ("d", i)
        if self.dma_val[i] > 0 and need.get(key, -1) < self.dma_val[i]:
            need[key] = self.dma_val[i]
        waits = self._emit_waits(queue, need)
        self.dma_val[i] += 16
        tag = (key, self.dma_val[i])
        self.streams[queue].append((waits, fn, (self.dma_sems[i], 16)))
        self._update(reads, writes, tag)

    def dma_bg(self, fn, queue="act"):
        self.bg_val += 16
        self.streams[queue].append(([], fn, (self.bg_sem, 16)))

    def barrier(self):
        need = {}
        for k in COMPUTE:
            if self.count[k] > 0:
                need[k] = self.count[k]
        for i, v in enumerate(self.dma_val):
            if v > 0:
                need[("d", i)] = v
        for eng in self.streams:
            waits = self._emit_waits(eng, dict(need), skip_self=(eng == "pe"))
            if waits:
                self.streams[eng].append((waits, None, None))

    def finish(self, eng="sp"):
        need = {}
        for k in COMPUTE:
            if self.count[k] > 0:
                need[k] = self.count[k]
        for i, v in enumerate(self.dma_val):
            if v > 0:
                need[("d", i)] = v
        if self.bg_val > 0:
            need["bg"] = self.bg_val
        waits = self._emit_waits(eng, need)
        self.streams[eng].append((waits, None, None))

    def emit(self):
        nc = self.nc
        streams = self.streams

        def run(engobj, lst):
            for waits, fn, inc in lst:
                for s, v in waits:
                    engobj.wait_ge(s, v)
                if fn is None:
                    continue
                ins = fn(engobj)
                if inc is not None:
                    ins.then_inc(inc[0], inc[1])

        with nc.Block() as block:
            @block.sync
            def _(e):
                run(e, streams["sp"])

            @block.tensor
            def _(e):
                run(e, streams["pe"])

            @block.scalar
            def _(e):
                run(e, streams["act"])

            @block.vector
            def _(e):
                run(e, streams["dve"])

            @block.gpsimd
            def _(e):
                run(e, streams["pool"])


class Tile:
    def __init__(self, t, name):
        self.t = t
        self.b = Buf(name)

    def __getitem__(self, k):
        return self.t[k]


class Builder:
    def __init__(self):
        self.nc = bass.Bass("TRN2", target_bir_lowering=False)
        self.fw = FW(self.nc)
        self.root = contextlib.ExitStack()
        self.ins = {}
        self.outs = {}
        self.uid = 0
        self.psum_tiles = []
        self.psum_rr = 0
        self.held = set()

    def din(self, name, shape):
        self.ins[name] = self.nc.dram_tensor(name, list(shape), F32, kind="ExternalInput").ap()
        return self.ins[name]

    def dout(self, name, shape):
        self.outs[name] = self.nc.dram_tensor(name, list(shape), F32, kind="ExternalOutput").ap()
        return self.outs[name]

    def sb(self, stack, name, shape, dt=F32):
        self.uid += 1
        t = stack.enter_context(self.nc.sbuf_tensor("%s_%d" % (name, self.uid), list(shape), dt))
        return Tile(t, name)

    def init_psum(self):
        for i in range(8):
            t = self.root.enter_context(self.nc.psum_tensor("psb%d" % i, [128, 512], F32))
            tl = Tile(t, "ps%d" % i)
            tl.b.excl = True
            self.psum_tiles.append(tl)

    def ps(self):
        while True:
            t = self.psum_tiles[self.psum_rr]
            self.psum_rr = (self.psum_rr + 1) % 8
            if id(t) not in self.held:
                return t

    def hold(self, t):
        self.held.add(id(t))

    def release(self, t):
        self.held.discard(id(t))

    def dma(self, out, in_, reads=(), writes=(), queue="sp"):
        self.fw.dma(lambda e: e.dma_start(out=out, in_=in_), reads=reads, writes=writes, queue=queue)

    def mm(self, out, lhsT, rhs, start, stop, reads, writes, inc):
        self.fw.op("pe", lambda e: e.matmul(out, lhsT=lhsT, rhs=rhs, start=start, stop=stop, skip_group_check=True),
                   reads=reads, writes=writes, inc=inc)

    def mm_group(self, out, pairs, reads, pst, last_inc=True):
        n = len(pairs)
        for i, (l, r) in enumerate(pairs):
            self.mm(out, l, r, i == 0, i == n - 1, reads, [pst.b], inc=(last_inc and i == n - 1))

    def transpose(self, out, in_, ident, reads, pst, inc=True):
        self.fw.op("pe", lambda e: e.transpose(out, in_, ident), reads=reads, writes=[pst.b], inc=inc)

    def act(self, out, in_, func, reads, writes, bias=None, scale=None, accum_out=None):
        kw = {}
        if bias is not None:
            kw["bias"] = bias
        if scale is not None:
            kw["scale"] = scale
        if accum_out is not None:
            kw["accum_out"] = accum_out
        self.fw.op("act", lambda e: e.activation(out=out, in_=in_, func=func, **kw), reads=reads, writes=writes)

    def tt(self, eng, out, in0, in1, op, reads, writes):
        self.fw.op(eng, lambda e: e.tensor_tensor(out=out, in0=in0, in1=in1, op=op), reads=reads, writes=writes)

    def ts(self, eng, out, in0, s1, s2, op0, op1, reads, writes):
        if s2 is None:
            self.fw.op(eng, lambda e: e.tensor_scalar(out=out, in0=in0, scalar1=s1, scalar2=None, op0=op0),
                       reads=reads, writes=writes)
        else:
            self.fw.op(eng, lambda e: e.tensor_scalar(out=out, in0=in0, scalar1=s1, scalar2=s2, op0=op0, op1=op1),
                       reads=reads, writes=writes)

    def stt(self, eng, out, in0, scalar, in1, op0, op1, reads, writes):
        self.fw.op(eng, lambda e: e.scalar_tensor_tensor(out=out, in0=in0, scalar=scalar, in1=in1, op0=op0, op1=op1),
                   reads=reads, writes=writes)

    def copy(self, eng, out, in_, reads, writes):
        if eng == "act":
            self.fw.op("act", lambda e: e.copy(out=out, in_=in_), reads=reads, writes=writes)
        else:
            self.fw.op(eng, lambda e: e.tensor_copy(out=out, in_=in_), reads=reads, writes=writes)

    def memset(self, eng, ap, val, writes):
        self.fw.op(eng, lambda e: e.memset(ap, val), writes=writes)

    def recip(self, out, in_, reads, writes):
        self.fw.op("dve", lambda e: e.reciprocal(out=out, in_=in_), reads=reads, writes=writes)

    def reduce(self, out, in_, op, reads, writes):
        self.fw.op("dve", lambda e: e.tensor_reduce(out=out, in_=in_, axis=AX.X, op=op), reads=reads, writes=writes)


def build_program():
    B = Builder()
    nc, fw = B.nc, B.fw
    B.init_psum()
    root = B.root

    xp = B.din("xp", [SEQ, D])
    xs = B.din("xs", [128, D])
    cp = B.din("cp", [128, D])
    cs = B.din("cs", [128, D])
    flag = B.din("flag", [128, 2])
    kc = [B.din("kc%d" % g, [NB_S, WINS[g], 1024]) for g in range(3)]
    sst = B.din("sst", [NB_S, 2048, 128])
    scv = B.din("scv", [NB_S * 3, 3072])
    w_ada = B.din("w_ada", [D, 3 * D])
    b_ada = B.din("b_ada", [1, 3 * D])
    w_in = B.din("w_in", [D, IN_W])
    conv_w = B.din("conv_w", [4, 3072])
    conv_b = B.din("conv_b", [1, 3072])
    dt_bias = B.din("dt_bias", [1, 32])
    a_log = B.din("a_log", [1, 32])
    d_skip = B.din("d_skip", [1, 32])
    ssm_norm_g = B.din("ssm_norm_g", [1, 2048])
    w_att = B.din("w_att", [512, D])
    w_ssm = B.din("w_ssm", [2048, D])
    w_out = B.din("w_out", [D, D])
    norm_g = B.din("norm_g", [1, D])
    fnorm_g = B.din("fnorm_g", [1, D])
    ident_d = B.din("ident", [128, 128])
    for nm, shp in (("c_tri", [128, 128]), ("c_triS", [128, 128]), ("c_blk", [128, 128]), ("c_mb4", [128, 512]),
                    ("c_mb4S", [128, 512]), ("c_selA", [8, 1024]), ("c_selB", [8, 1024]), ("c_bmask", [128, 2048]),
                    ("c_rowmask", [128, 16]), ("c_abias", [4, 128, 1536]), ("c_sbias", [4, 128, 768]),
                    ("c_cbias", [128, 832]), ("c_hmask", [128, 4])):
        B.din(nm, shp)

    yp = B.dout("yp", [HALF, D])
    ys = B.dout("ys", [128, D])
    kvp = [B.dout("kvp%d" % g, [WINS[g], 1024]) for g in range(3)]
    ssmp = B.dout("ssmp", [2048, 128])
    convp = B.dout("convp", [3, 3072])
    kvs = [B.dout("kvs%d" % g, [NB_S, WINS[g], 1024]) for g in range(3)]
    ssms = B.dout("ssms", [NB_S, 2048, 128])
    convs = B.dout("convs", [NB_S * 3, 3072])

    def issue_cache_shift():
        for g in range(3):
            Lc = WINS[g]
            for b in range(NB_S):
                o = kvs[g][b, 0:Lc - 8, :].rearrange("(a r) n -> a (r n)", a=8)
                i = kc[g][b, 8:Lc, :].rearrange("(a r) n -> a (r n)", a=8)
                fw.dma_bg((lambda o, i: (lambda e: e.dma_start(out=o, in_=i)))(o, i), queue="sp")

    ident_f = B.sb(root, "ident_f", [128, 128], F32)
    ident_b = B.sb(root, "ident_b", [128, 128], BF16)
    hTo = B.sb(root, "hTo", [128, 8, HALF + 128], BF16)
    halo_scope = contextlib.ExitStack()
    hTh = B.sb(halo_scope, "hTh", [128, 8, HALF], BF16)

    def H(k, a, n, step=1):
        t, o = (hTh, a) if a < HALF else (hTo, a - HALF)
        if step == 1:
            return t[:, k, o:o + n]
        return t[:, k, o:o + (n - 1) * step + 1:step]

    def Hb(a):
        return hTh.b if a < HALF else hTo.b

    def Hall(a, n):
        t, o = (hTh, a) if a < HALF else (hTo, a - HALF)
        return t[:, :, o:o + n]
    gate_d = nc.dram_tensor("gate_scratch", [2, 128, D], F32, kind="Internal").ap()
    gate_db = Buf("gate_d")
    vnew_d = nc.dram_tensor("vnew_scratch", [128, 3 * 512], BF16, kind="Internal").ap()
    vnew_db = Buf("vnew_d")
    attT_d = nc.dram_tensor("attT_scratch", [512, HALF + 128], BF16, kind="Internal").ap()
    attT_db = Buf("attT_d")
    B.dma(ident_f[:], ident_d, writes=[ident_f.b])
    B.dma(ident_b[:], ident_d, writes=[ident_b.b], queue="pool")

    with contextlib.ExitStack() as ph:
        wada = B.sb(ph, "wada", [128, 8, 3 * D], BF16)
        for kcx in range(8):
            B.dma(wada[:, kcx, :], w_ada[kcx * 128:(kcx + 1) * 128, :], writes=[wada.b], queue="pool")
        bada = B.sb(ph, "bada", [128, 3 * D], F32)
        B.dma(bada[:], b_ada[0:1, :].partition_broadcast(128), writes=[bada.b])
        ngb = B.sb(ph, "ngb", [128, D], F32)
        B.dma(ngb[:], norm_g[0:1, :].partition_broadcast(128), writes=[ngb.b])
        mods = []
        for which, cdram, gidx in (("P", cp, 0), ("S", cs, 1)):
            c32 = B.sb(ph, "c32" + which, [128, D], F32)
            B.dma(c32[:], cdram, writes=[c32.b])
            c16 = B.sb(ph, "c16" + which, [128, D], BF16)
            B.act(c16[:], c32[:], AF.Silu, [c32.b], [c16.b])
            cT = B.sb(ph, "cT" + which, [128, 8, 128], BF16)
            pt = B.ps()
            ptb = pt.t[:].bitcast(BF16)
            for kcx in range(8):
                B.transpose(ptb[:, kcx * 128:(kcx + 1) * 128], c16[:, kcx * 128:(kcx + 1) * 128], ident_b[:],
                            [c16.b, ident_b.b], pt, inc=(kcx == 7))
            B.copy("dve", cT[:].rearrange("p k t -> p (k t)"), ptb[:, 0:1024], [pt.b], [cT.b])
            mod = B.sb(ph, "mod" + which, [128, 3 * D], F32)
            for nb in range(6):
                pt = B.ps()
                B.mm_group(pt.t[:], [(cT[:, kcx, :], wada[:, kcx, nb * 512:(nb + 1) * 512]) for kcx in range(8)],
                           [cT.b, wada.b], pt)
                B.tt("dve", mod[:, nb * 512:(nb + 1) * 512], pt.t[:], bada[:, nb * 512:(nb + 1) * 512], ALU.add,
                     [pt.b, bada.b], [mod.b])
            B.stt("dve", mod[:, D:2 * D], mod[:, D:2 * D], 1.0, ngb[:], ALU.add, ALU.mult, [mod.b, ngb.b], [mod.b])
            B.dma(gate_d[gidx], mod[:, 2 * D:3 * D], reads=[mod.b], writes=[gate_db])
            mods.append(mod)

        issue_cache_shift()
        xbufs = [B.sb(ph, "xin%d" % i, [128, D], F32) for i in range(3)]
        h1s = [B.sb(ph, "h1_%d" % i, [128, D], F32) for i in range(2)]
        h2s = [B.sb(ph, "h2_%d" % i, [128, D], BF16) for i in range(2)]
        junk = B.sb(ph, "junk", [128, D], F32)
        stat = [B.sb(ph, "stat%d" % i, [128, 4], F32) for i in range(2)]
        for ti in range(33):
            xt = xbufs[ti % 3]
            h1 = h1s[ti % 2]
            h2 = h2s[ti % 2]
            sT = stat[ti % 2]
            mod = mods[0] if ti < 32 else mods[1]
            src = xp[ti * 128:(ti + 1) * 128, :] if ti < 32 else xs
            B.dma(xt[:], src, writes=[xt.b])
            B.act(junk[:], xt[:], AF.Square, [xt.b], [junk.b, sT.b], accum_out=sT[:, 0:1])
            B.act(sT[:, 1:2], sT[:, 0:1], AF.Ln, [sT.b], [sT.b], bias=EPS, scale=1.0 / D)
            B.act(sT[:, 2:3], sT[:, 1:2], AF.Exp, [sT.b], [sT.b], scale=-0.5)
            B.stt("dve", h1[:], xt[:], sT[:, 2:3], mod[:, D:2 * D], ALU.mult, ALU.mult, [xt.b, sT.b, mod.b], [h1.b])
            B.tt("dve", h2[:], h1[:], mod[:, 0:D], ALU.add, [h1.b, mod.b], [h2.b])
            pt = B.ps()
            ptb = pt.t[:].bitcast(BF16)
            for kcx in range(8):
                B.transpose(ptb[:, kcx * 128:(kcx + 1) * 128], h2[:, kcx * 128:(kcx + 1) * 128], ident_b[:],
                            [h2.b, ident_b.b], pt, inc=(kcx == 7))
            B.copy("act", Hall(ti * 128, 128), ptb[:, 0:1024].rearrange("p (k t) -> p k t", k=8),
                   [pt.b], [Hb(ti * 128)])

    fw.barrier()
    with contextlib.ExitStack() as ph:
        wkv = [B.sb(ph, "wkv%d" % i, [128, 8, 1024], BF16) for i in range(2)]
        obuf = [B.sb(ph, "kvo%d" % i, [128, 1024], F32) for i in range(3)]
        vtmps = [B.sb(ph, "vtmp%d" % i, [128, 512], BF16) for i in range(2)]
        oi = 0
        for g in range(3):
            w = wkv[g % 2]
            B.dma(w[:, :, 0:512], w_in[:, OFF_K + g * 512:OFF_K + (g + 1) * 512].rearrange("(k p) n -> p k n", p=128),
                  writes=[w.b], queue="pool")
            B.dma(w[:, :, 512:1024], w_in[:, OFF_V + g * 512:OFF_V + (g + 1) * 512].rearrange("(k p) n -> p k n", p=128),
                  writes=[w.b], queue="pool")
            ntile = WINS[g] // 128
            tiles = [(32 - ntile + i, i) for i in range(ntile)] + [(32, None)]
            for (ti, orow) in tiles:
                ob = obuf[oi % 3]
                oi += 1
                for half in range(2):
                    pt = B.ps()
                    B.mm_group(pt.t[:], [(H(kcx, ti * 128, 128), w[:, kcx, half * 512:(half + 1) * 512])
                                         for kcx in range(8)], [Hb(ti * 128), w.b], pt)
                    B.copy("act" if half == 0 else "dve", ob[:, half * 512:(half + 1) * 512], pt.t[:], [pt.b], [ob.b])
                if orow is not None:
                    B.dma(kvp[g][orow * 128:(orow + 1) * 128, :], ob[:], reads=[ob.b])
                else:
                    Lc = WINS[g]
                    for b in range(NB_S):
                        B.dma(kvs[g][b, Lc - 8:Lc, :], ob[b * 8:(b + 1) * 8, :], reads=[ob.b])
                    vtmp = vtmps[g % 2]
                    B.copy("pool", vtmp[:], ob[:, 512:1024], [ob.b], [vtmp.b])
                    B.dma(vnew_d[:, g * 512:(g + 1) * 512], vtmp[:], reads=[vtmp.b], writes=[vnew_db])

        wx = [B.sb(ph, "wx%d" % i, [128, 8, 512], BF16) for i in range(2)]
        cvo = [B.sb(ph, "cvo%d" % i, [128, 3072], F32) for i in range(2)]
        for blk in range(6):
            w = wx[blk % 2]
            B.dma(w[:], w_in[:, OFF_XBC + blk * 512:OFF_XBC + (blk + 1) * 512].rearrange("(k p) n -> p k n", p=128),
                  writes=[w.b], queue="pool")
            for j, ti in enumerate((31, 32)):
                pt = B.ps()
                B.mm_group(pt.t[:], [(H(kcx, ti * 128, 128), w[:, kcx, :]) for kcx in range(8)],
                           [Hb(ti * 128), w.b], pt)
                B.copy("act" if j == 0 else "dve", cvo[j][:, blk * 512:(blk + 1) * 512], pt.t[:], [pt.b], [cvo[j].b])
        B.dma(convp[0:3, :], cvo[0][125:128, :], reads=[cvo[0].b])
        for b in range(NB_S):
            B.dma(convs[b * 3:(b + 1) * 3, :], cvo[1][b * 8 + 5:b * 8 + 8, :], reads=[cvo[1].b])

    ynT_d = nc.dram_tensor("ynT_scratch", [2048, HALF + 128], BF16, kind="Internal").ap()
    ynT_db = Buf("ynT_d")

    def ssd_phase():
        with contextlib.ExitStack() as ph:
            def cload(name, shape, dt_=F32, src=None):
                t = B.sb(ph, name, shape, dt_)
                B.dma(t[:] if len(shape) == 2 else t[:].rearrange("p a b -> p (a b)"), B.ins[src or ("c_" + name)],
                      writes=[t.b], queue=("pool" if dt_ == BF16 else "sp"))
                return t
            triU = cload("tri", [128, 128])
            triS = cload("triS", [128, 128])
            blkS = cload("blk", [128, 128])
            mb4 = cload("mb4", [128, 512], BF16)
            mb4S = cload("mb4S", [128, 512], BF16)
            selA = cload("selA", [8, 1024])
            selB = cload("selB", [8, 2, 512])
            bmask = cload("bmask", [128, 16, 128], BF16)
            rowmask = cload("rowmask", [128, 16])
            onesF = B.sb(ph, "onesF", [128, 128], F32)
            B.memset("pool", onesF[:], 1.0, [onesF.b])
            flg = B.sb(ph, "flg", [128, 2], F32)
            B.dma(flg[:], flag, writes=[flg.b])
            dtb = B.sb(ph, "dtb", [128, 32], F32)
            B.dma(dtb[:], dt_bias[0:1, :].partition_broadcast(128), writes=[dtb.b])
            abc = B.sb(ph, "abc", [128, 32], F32)
            B.dma(abc[:], a_log[0:1, :].partition_broadcast(128), writes=[abc.b])
            B.act(abc[:], abc[:], AF.Exp, [abc.b], [abc.b])
            B.ts("dve", abc[:], abc[:], -1.0, None, ALU.mult, None, [abc.b], [abc.b])
            dsk = B.sb(ph, "dsk", [128, 32], F32)
            B.dma(dsk[:], d_skip[0:1, :].partition_broadcast(128), writes=[dsk.b])
            scvS = B.sb(ph, "scvS", [48, 6, 128], F32)

            wgs = [B.sb(ph, "wg%d" % i, [128, 8, 1288], BF16) for i in range(1)]
            gnb = [B.sb(ph, "gnb%d" % i, [128, 512], F32) for i in range(1)]
            xraw = [B.sb(ph, "xraw%d" % i, [128, 6, 515], BF16) for i in range(1)]
            hist = B.sb(ph, "hist", [128, 6, 3], BF16)
            dg = B.sb(ph, "dg", [128, 6, 4, 128], BF16)
            cwT = B.sb(ph, "cwT", [128, 24, 8], F32)
            class _V2:
                def __init__(self, ap, b):
                    self.ap, self.b = ap, b

                def __getitem__(self, k):
                    return self.ap[k]
            cw5 = B.sb(ph, "cw5", [5, 512], F32)
            for q in range(6):
                B.dma(cw5[0:4, :], conv_w[:, q * 512:(q + 1) * 512], writes=[cw5.b])
                B.dma(cw5[4:5, :], conv_b[:, q * 512:(q + 1) * 512], writes=[cw5.b])
                pt = B.ps()
                for j in range(4):
                    ci = q * 4 + j
                    B.transpose(pt.t[:, j * 8:j * 8 + 5], cw5[:, j * 128:(j + 1) * 128], ident_f[0:5, 0:5],
                                [cw5.b, ident_f.b], pt, inc=(j == 3))
                B.copy("dve", cwT[:, q * 4:(q + 1) * 4, 0:5], pt.t[:, 0:32].rearrange("p (j e) -> p j e", e=8)[:, :, 0:5],
                       [pt.b], [cwT.b])
            ctmp = [B.sb(ph, "ctmp%d" % i, [128, 512], F32) for i in range(2)]
            xcs = [B.sb(ph, "xc%d" % i, [128, 6, 512], BF16) for i in range(2)]
            xtoks = [B.sb(ph, "xtok%d" % i, [128, 4, 512], BF16) for i in range(2)]
            btoks = [B.sb(ph, "btok%d" % i, [128, 4, 128], BF16) for i in range(2)]
            S32 = B.sb(ph, "S32", [128, 512], F32)
            Sbf = [B.sb(ph, "Sbf%d" % i, [128, 512], BF16) for i in range(5)]
            ynTs = [B.sb(ph, "ynT%d" % i, [128, 4, 512], BF16) for i in range(2)]
            R = {}
            for nm, shp, dt_ in (("dts", [128, 8], F32), ("da", [128, 8], F32), ("cs", [128, 8], F32),
                                 ("d2e", [128, 8], F32), ("ecs", [128, 8], F32), ("etot", [128, 8], F32),
                                 ("wl", [128, 8], F32), ("csT", [8, 128], F32), ("ncsT", [8, 128], F32),
                                 ("xdt", [128, 512], BF16), ("xdtw", [128, 512], BF16), ("xsD", [128, 512], BF16),
                                 ("cbm", [128, 128], BF16), ("E", [128, 2, 512], BF16), ("MT", [128, 2, 512], BF16),
                                 ("zs", [128, 512], F32), ("t1", [128, 512], F32),
                                 ("yn", [128, 512], BF16), ("st", [128, 4], F32), ("jk", [128, 512], F32)):
                nb_ = 4 if shp[-1] <= 128 and len(shp) == 2 else 2
                if nm in ("jk", "zs"):
                    nb_ = 1
                tl_ = [B.sb(ph, nm + "%d" % i, shp, dt_) for i in range(nb_)]
                R[nm] = [tl_[i % nb_] for i in range(4)]
            class _View:
                def __init__(self, ap, b):
                    self.ap, self.b = ap, b

                def __getitem__(self, k):
                    return self.ap[k]
            xpadS = _View(xraw[0][:, :, 0:176].rearrange("p c (b w) -> p c b w", w=11), xraw[0].b)
            xcS = _View(xcs[0][:, :, 0:128], xcs[0].b)
            xtokS = _View(xtoks[0][:, 0, :], xtoks[0].b)
            btokS = _View(btoks[0][:, 0, :], btoks[0].b)
            CTm = B.sb(ph, "CTm", [128, 16, 128], BF16)
            inits = [B.sb(ph, "init%d" % i, [128, 4, 128], F32) for i in range(2)]
            initT = [B.sb(ph, "initT%d" % i, [128, 512], BF16) for i in range(1)]
            initT = [initT[0], initT[0]]
            fins = [B.sb(ph, "fin%d" % i, [128, 4, 128], F32) for i in range(2)]
            bmk = [B.sb(ph, "bmk%d" % i, [128, 128], BF16) for i in range(1)]
            bmk = [bmk[0], bmk[0]]
            darep = ctmp[0]
            etc = B.sb(ph, "etc", [128, 4, 16], F32)
            ynTS = _View(ynTs[0][:, :, 0:128], ynTs[0].b)
            rr = {"i": 0, "s": 0, "c": 0}

            def bc8(t):
                return t.unsqueeze(2).to_broadcast([128, 8, 64])

            def v8(ap):
                return ap.rearrange("p (h d) -> p h d", h=8)

            def chunk(g, wg, gn, xs_ap, xs_b, bt_ap, bt_b, BT, CT, xc_b, hcols, own, sample, yn_dst, yn_b):
                i = rr["i"] % 4
                rr["i"] += 1
                r = {k: v[i] for k, v in R.items()}
                tri = triS if sample else triU
                ones_ = blkS if sample else onesF
                mb = mb4S if sample else mb4
                pdt = B.ps()
                B.mm_group(pdt.t[:, 0:8], [(H(k, hcols, 128), wg[:, k, 1280:1288]) for k in range(8)], [Hb(hcols), wg.b], pdt)
                B.tt("dve", r["dts"][:], pdt.t[:, 0:8], dtb[:, g * 8:(g + 1) * 8], ALU.add, [pdt.b, dtb.b], [r["dts"].b])
                B.act(r["dts"][:], r["dts"][:], AF.Exp, [r["dts"].b], [r["dts"].b])
                B.act(r["dts"][:], r["dts"][:], AF.Ln, [r["dts"].b], [r["dts"].b], bias=1.0)
                B.tt("dve", r["da"][:], r["dts"][:], abc[:, g * 8:(g + 1) * 8], ALU.mult, [r["dts"].b, abc.b], [r["da"].b])
                pcs = B.ps()
                B.mm(pcs.t[:, 0:8], tri[:], r["da"][:], True, True, [tri.b, r["da"].b], [pcs.b], True)
                ptot = B.ps()
                B.mm(ptot.t[:, 0:8], ones_[:], r["da"][:], True, True, [ones_.b, r["da"].b], [ptot.b], True)
                B.copy("act", r["cs"][:], pcs.t[:, 0:8], [pcs.b], [r["cs"].b])
                B.tt("dve", r["d2e"][:], ptot.t[:, 0:8], r["cs"][:], ALU.subtract, [ptot.b, r["cs"].b], [r["d2e"].b])
                B.act(r["d2e"][:], r["d2e"][:], AF.Exp, [r["d2e"].b], [r["d2e"].b])
                B.act(r["etot"][:], ptot.t[:, 0:8], AF.Exp, [ptot.b], [r["etot"].b])
                B.tt("dve", r["wl"][:], r["dts"][:], r["d2e"][:], ALU.mult, [r["dts"].b, r["d2e"].b], [r["wl"].b])
                B.tt("dve", v8(r["xdtw"][:]), v8(xs_ap), bc8(r["wl"][:]), ALU.mult, [xs_b, r["wl"].b], [r["xdtw"].b])
                sprev = Sbf[rr["s"] % 5]
                if not sample:
                    pst = B.ps()
                    B.mm(pst.t[:], bt_ap, r["xdtw"][:], True, True, [bt_b, r["xdtw"].b], [pst.b], True)
                    B.tt("dve", v8(S32[:]), v8(S32[:]), bc8(r["etot"][:]), ALU.mult, [S32.b, r["etot"].b], [S32.b])
                    B.tt("dve", S32[:], S32[:], pst.t[:], ALU.add, [S32.b, pst.b], [S32.b])
                    rr["s"] += 1
                    B.copy("act", Sbf[rr["s"] % 5][:], S32[:], [S32.b], [Sbf[rr["s"] % 5].b])
                if not own:
                    return None

                def body():
                    chunk_body(g, wg, gn, xs_ap, xs_b, bt_ap, bt_b, BT, CT, xc_b, hcols, sample, yn_dst, yn_b, r, sprev, tri, mb)
                return body

            def chunk_body(g, wg, gn, xs_ap, xs_b, bt_ap, bt_b, BT, CT, xc_b, hcols, sample, yn_dst, yn_b, r, sprev, tri, mb):
                B.act(r["ecs"][:], r["cs"][:], AF.Exp, [r["cs"].b], [r["ecs"].b])
                pyo = B.ps()
                B.hold(pyo)
                if not sample:
                    B.mm(pyo.t[:], CT, sprev[:], True, True, [xc_b, sprev.b], [pyo.b], True)
                else:
                    B.tt("dve", CTm[:], CT.unsqueeze(1).to_broadcast([128, 16, 128]), bmask[:], ALU.mult,
                         [xc_b, bmask.b], [CTm.b])
                    B.copy("dve", v8(darep[:]), bc8(r["da"][:]), [r["da"].b], [darep.b])
                    for ci in range(4):
                        pe_ = B.ps()
                        B.mm(pe_.t[:, 0:16], darep[:, ci * 128:(ci + 1) * 128], rowmask[:], True, True,
                             [darep.b, rowmask.b], [pe_.b], True)
                        B.act(etc[:, ci, :], pe_.t[:, 0:16], AF.Exp, [pe_.b], [etc.b])
                    for b in range(NB_S):
                        ini = inits[b % 2]
                        iT = initT[b % 2]
                        B.dma(ini[:], sst[b, g * 512:(g + 1) * 512, :].rearrange("(c p) n -> p c n", p=128), writes=[ini.b])
                        ptr = B.ps()
                        for ci in range(4):
                            B.transpose(ptr.t[:, ci * 128:(ci + 1) * 128], ini[:, ci, :], ident_f[:],
                                        [ini.b, ident_f.b], ptr, inc=(ci == 3))
                        B.copy("act", iT[:], ptr.t[:], [ptr.b], [iT.b])
                        B.mm(pyo.t[:], CTm[:, b, :], iT[:], b == 0, b == NB_S - 1, [CTm.b, iT.b], [pyo.b], b == NB_S - 1)
                        bm = bmk[b % 2]
                        B.ts("dve", bm[:], bt_ap, rowmask[:, b:b + 1], None, ALU.mult, None, [bt_b, rowmask.b], [bm.b])
                        pfs = B.ps()
                        for ci in range(4):
                            B.mm(pfs.t[:, ci * 128:(ci + 1) * 128], r["xdtw"][:, ci * 128:(ci + 1) * 128], bm[:],
                                 ci == 0, ci == 3, [r["xdtw"].b, bm.b], [pfs.b], ci == 3)
                        fn_ = fins[b % 2]
                        for ci in range(4):
                            B.stt("dve", fn_[:, ci, :], ini[:, ci, :], etc[:, ci, b:b + 1], pfs.t[:, ci * 128:(ci + 1) * 128],
                                  ALU.mult, ALU.add, [ini.b, etc.b, pfs.b], [fn_.b])
                        B.dma(ssms[b, g * 512:(g + 1) * 512, :].rearrange("(c p) n -> p c n", p=128), fn_[:], reads=[fn_.b])
                B.copy("act", r["t1"][:], pyo.t[:], [pyo.b], [r["t1"].b])
                B.release(pyo)
                B.tt("dve", v8(r["t1"][:]), v8(r["t1"][:]), bc8(r["ecs"][:]), ALU.mult, [r["t1"].b, r["ecs"].b], [r["t1"].b])
                pT = B.ps()
                B.mm(pT.t[0:8, 0:128], r["da"][:], tri[:], True, True, [r["da"].b, tri.b], [pT.b], True)
                B.copy("act", r["csT"][:], pT.t[0:8, 0:128], [pT.b], [r["csT"].b])
                B.ts("dve", r["ncsT"][:], pT.t[0:8, 0:128], -1.0, None, ALU.mult, None, [pT.b], [r["ncsT"].b])
                B.tt("dve", v8(r["xdt"][:]), v8(xs_ap), bc8(r["dts"][:]), ALU.mult, [xs_b, r["dts"].b], [r["xdt"].b])
                B.tt("dve", v8(r["xsD"][:]), v8(xs_ap), bc8(dsk[:, g * 8:(g + 1) * 8]), ALU.mult,
                     [xs_b, dsk.b], [r["xsD"].b])
                pcb = B.ps()
                B.mm(pcb.t[:, 0:128], BT, CT, True, True, [xc_b], [pcb.b], True)
                B.tt("dve", r["cbm"][:], pcb.t[:, 0:128], tri[:], ALU.mult, [pcb.b, tri.b], [r["cbm"].b])
                for bk in range(2):
                    pseg = B.ps()
                    B.mm(pseg.t[:], r["ncsT"][:], selB[:, bk, :], True, False, [r["ncsT"].b, selB.b], [pseg.b], False)
                    for j in range(4):
                        h = bk * 4 + j
                        B.mm(pseg.t[:, j * 128:(j + 1) * 128], selA[:, h * 128:(h + 1) * 128], r["csT"][:], False, False,
                             [selA.b, r["csT"].b], [pseg.b], False)
                    B.mm(pseg.t[:], ident_b[:], mb[:], False, True, [ident_b.b, mb.b], [pseg.b], True)
                    B.act(r["E"][:, bk, :], pseg.t[:], AF.Exp, [pseg.b], [r["E"].b])
                    B.tt("dve", r["MT"][:, bk, :].rearrange("p (h l) -> p h l", h=4),
                         r["E"][:, bk, :].rearrange("p (h l) -> p h l", h=4),
                         r["cbm"][:].unsqueeze(1).to_broadcast([128, 4, 128]), ALU.mult,
                         [r["E"].b, r["cbm"].b], [r["MT"].b])
                pyd = B.ps()
                for h in range(8):
                    B.mm(pyd.t[:, h * 64:(h + 1) * 64], r["MT"][:, h // 4, (h % 4) * 128:(h % 4 + 1) * 128],
                         r["xdt"][:, h * 64:(h + 1) * 64], h == 0, False, [r["MT"].b, r["xdt"].b], [pyd.b], False)
                B.mm(pyd.t[:], ident_b[:], r["xsD"][:], False, True, [ident_b.b, r["xsD"].b], [pyd.b], True)
                B.tt("dve", r["t1"][:], r["t1"][:], pyd.t[:], ALU.add, [r["t1"].b, pyd.b], [r["t1"].b])
                pz = B.ps()
                B.mm_group(pz.t[:], [(H(k, hcols, 128), wg[:, k, 768:1280]) for k in range(8)], [Hb(hcols), wg.b], pz)
                B.act(r["zs"][:], pz.t[:], AF.Silu, [pz.b], [r["zs"].b])
                B.tt("dve", r["t1"][:], r["t1"][:], r["zs"][:], ALU.mult, [r["t1"].b, r["zs"].b], [r["t1"].b])
                B.act(r["jk"][:], r["t1"][:], AF.Square, [r["t1"].b], [r["jk"].b, r["st"].b], accum_out=r["st"][:, 0:1])
                B.act(r["st"][:, 1:2], r["st"][:, 0:1], AF.Ln, [r["st"].b], [r["st"].b], bias=EPS, scale=1.0 / 512)
                B.act(r["st"][:, 2:3], r["st"][:, 1:2], AF.Exp, [r["st"].b], [r["st"].b], scale=-0.5)
                B.stt("dve", r["yn"][:], r["t1"][:], r["st"][:, 2:3], gn[:], ALU.mult, ALU.mult,
                      [r["t1"].b, r["st"].b, gn.b], [r["yn"].b])
                pyt = B.ps()
                pytb = pyt.t[:].bitcast(BF16)
                for ci in range(4):
                    B.transpose(pytb[:, ci * 128:(ci + 1) * 128], r["yn"][:, ci * 128:(ci + 1) * 128], ident_b[:],
                                [r["yn"].b, ident_b.b], pyt, inc=(ci == 3))
                B.copy("act", yn_dst, pytb[:, 0:512].rearrange("p (c t) -> p c t", c=4), [pyt.b], [yn_b])

            def conv_silu(g, src, srcb, cidx, nfree, dst, dstb, view):
                tmp = ctmp[rr["c"] % 2]
                rr["c"] += 1
                tv = view(tmp)
                wv = cwT[:, cidx, :]
                B.act(tv, src(3), AF.Identity, [srcb, cwT.b], [tmp.b], bias=wv[:, 4:5], scale=wv[:, 3:4])
                B.stt("dve", tv, src(2), wv[:, 2:3], tv, ALU.mult, ALU.add, [srcb, cwT.b, tmp.b], [tmp.b])
                B.stt("dve", tv, src(1), wv[:, 1:2], tv, ALU.mult, ALU.add, [srcb, cwT.b, tmp.b], [tmp.b])
                B.stt("dve", tv, src(0), wv[:, 0:1], tv, ALU.mult, ALU.add, [srcb, cwT.b, tmp.b], [tmp.b])
                B.act(dst, tv, AF.Silu, [tmp.b], [dstb])

            def chan_idx(g, ci):
                return g * 4 + ci if ci < 4 else (16 + g if ci == 4 else 20 + g)

            def wcol(ci):
                return slice(ci * 128, (ci + 1) * 128)

            for g in range(4):
                wg = wgs[0]
                gn = gnb[0]
                xo = OFF_XBC
                for (c0, n, src0) in ((0, 512, xo + g * 512), (512, 128, xo + 2048 + g * 128), (640, 128, xo + 2560 + g * 128),
                                      (768, 512, OFF_Z + g * 512), (1280, 8, OFF_DT + g * 8)):
                    B.dma(wg[:, :, c0:c0 + n], w_in[:, src0:src0 + n].rearrange("(k p) n -> p k n", p=128),
                          writes=[wg.b], queue="pool")
                B.dma(gn[:], ssm_norm_g[0:1, g * 512:(g + 1) * 512].partition_broadcast(128), writes=[gn.b])
                B.memset("dve", S32[:], 0.0, [S32.b])
                B.memset("pool", Sbf[rr["s"] % 5][:], 0.0, [Sbf[rr["s"] % 5].b])
                for ci in range(6):
                    for w in range(4):
                        B.ts("dve", dg[:, ci, w, :], ident_f[:], cwT[:, chan_idx(g, ci), w:w + 1], None,
                             ALU.mult, None, [ident_f.b, cwT.b], [dg.b])
                for blk in range(8):
                    own = blk >= 4
                    xr = xraw[0]
                    xc = xcs[blk % 2]
                    xt = xtoks[blk % 2]
                    bt = btoks[blk % 2]
                    ynb = ynTs[blk % 2]
                    nch = 6 if own else 5
                    if blk == 0:
                        B.memset("pool", xr[:, :, 0:3], 0.0, [xr.b])
                    elif blk == 4:
                        B.ts("dve", xr[:, :, 0:3], hist[:], flg[:, 0:1], None, ALU.mult, None, [hist.b, flg.b], [xr.b])
                    else:
                        nh = 6 if blk >= 5 else 5
                        B.copy("pool", xr[:, 0:nh, 0:3], hist[:, 0:nh, :], [hist.b], [xr.b])
                    for ci in range(6 if blk >= 3 else 5):
                        pt = B.ps()
                        B.mm_group(pt.t[:], [(wg[:, k, wcol(ci)], H(k, blk * 512, 512)) for k in range(8)],
                                   [wg.b, Hb(blk * 512)], pt)
                        B.copy("act" if ci % 2 == 0 else "dve", xr[:, ci, 3:515], pt.t[:], [pt.b], [xr.b])
                    for ci in range(nch):
                        pcv = B.ps()
                        for w in range(4):
                            B.mm(pcv.t[:], dg[:, ci, w, :], xr[:, ci, w:w + 512], w == 0, w == 3, [dg.b, xr.b], [pcv.b], w == 3)
                        B.act(xc[:, ci, :], pcv.t[:], AF.Silu, [pcv.b, cwT.b], [xc.b], bias=cwT[:, chan_idx(g, ci), 4:5])
                    nh = 6 if blk >= 3 else 5
                    B.copy("pool", hist[:, 0:nh, :], xr[:, 0:nh, 512:515], [xr.b], [hist.b])
                    if blk == 4:
                        B.ts("dve", S32[:], S32[:], flg[:, 0:1], None, ALU.mult, None, [S32.b, flg.b], [S32.b])
                        rr["s"] += 1
                        B.copy("act", Sbf[rr["s"] % 5][:], S32[:], [S32.b], [Sbf[rr["s"] % 5].b])
                    for half in range(2):
                        pt = B.ps()
                        ptb = pt.t[:].bitcast(BF16)
                        for tl in range(2):
                            for ci in range(4):
                                tcol = (half * 2 + tl) * 128
                                B.transpose(ptb[:, tl * 512 + ci * 128: tl * 512 + (ci + 1) * 128], xc[:, ci, tcol:tcol + 128],
                                            ident_b[:], [xc.b, ident_b.b], pt, inc=(tl == 1 and ci == 3))
                        B.copy("act" if half == 0 else "dve", xt[:, half * 2:half * 2 + 2, :],
                               ptb[:, 0:1024].rearrange("p (t c) -> p t c", t=2), [pt.b], [xt.b])
                    pt = B.ps()
                    ptb = pt.t[:].bitcast(BF16)
                    for tl in range(4):
                        B.transpose(ptb[:, tl * 128:(tl + 1) * 128], xc[:, 4, tl * 128:(tl + 1) * 128], ident_b[:],
                                    [xc.b, ident_b.b], pt, inc=(tl == 3))
                    B.copy("dve", bt[:].rearrange("p t n -> p (t n)"), ptb[:, 0:512], [pt.b], [bt.b])
                    bodies = []
                    for tl in range(4):
                        tcol = tl * 128
                        L0 = blk * 512 + tcol
                        bodies.append(chunk(g, wg, gn, xt[:, tl, :], xt.b, bt[:, tl, :], bt.b, xc[:, 4, tcol:tcol + 128],
                                            xc[:, 5, tcol:tcol + 128], xc.b, L0, own, False,
                                            ynb[:, :, tcol:tcol + 128], ynb.b))
                    for bd in bodies:
                        if bd is not None:
                            bd()
                    if own:
                        t0 = (blk - 4) * 512
                        B.dma(ynT_d[g * 512:(g + 1) * 512, t0:t0 + 512].rearrange("(c p) t -> p c t", p=128), ynb[:],
                              reads=[ynb.b], writes=[ynT_db])
                ptr = B.ps()
                for ci in range(4):
                    B.transpose(ptr.t[:, ci * 128:(ci + 1) * 128], S32[:, ci * 128:(ci + 1) * 128], ident_f[:],
                                [S32.b, ident_f.b], ptr, inc=(ci == 3))
                fo = fins[0]
                B.copy("dve", fo[:].rearrange("p c n -> p (c n)"), ptr.t[:], [ptr.b], [fo.b])
                B.dma(ssmp[g * 512:(g + 1) * 512, :].rearrange("(c p) n -> p c n", p=128), fo[:], reads=[fo.b])

                for ci in range(6):
                    cidx = chan_idx(g, ci)
                    B.dma(scvS[:, ci, :], scv[:, cidx * 128:(cidx + 1) * 128], writes=[scvS.b])
                for ci in range(6):
                    cidx = chan_idx(g, ci)
                    pt = B.ps()
                    B.transpose(pt.t[:, 0:48], scvS[:, ci, :], ident_f[0:48, 0:48],
                                [scvS.b, ident_f.b], pt)
                    B.copy("dve", xpadS[:, ci, :, 0:3], pt.t[:, 0:48].rearrange("p (b w) -> p b w", w=3), [pt.b], [xpadS.b])
                    pt = B.ps()
                    B.mm_group(pt.t[:, 0:128], [(wg[:, k, wcol(ci)], H(k, SEQ, 128)) for k in range(8)],
                               [wg.b, Hb(SEQ)], pt)
                    B.copy("act", xpadS[:, ci, :, 3:11], pt.t[:, 0:128].rearrange("p (b s) -> p b s", s=8), [pt.b], [xpadS.b])
                for ci in range(6):
                    conv_silu(g, (lambda k, ci=ci: xpadS[:, ci, :, k:k + 8]), xpadS.b, chan_idx(g, ci), 128,
                              xcS[:, ci, :].rearrange("p (b s) -> p b s", s=8), xcS.b,
                              lambda t: t[:, 0:128].rearrange("p (b s) -> p b s", s=8))
                pt = B.ps()
                ptb = pt.t[:].bitcast(BF16)
                for ci in range(5):
                    B.transpose(ptb[:, ci * 128:(ci + 1) * 128], xcS[:, ci, :], ident_b[:], [xcS.b, ident_b.b], pt, inc=(ci == 4))
                B.copy("act", xtokS[:], ptb[:, 0:512], [pt.b], [xtokS.b])
                B.copy("dve", btokS[:], ptb[:, 512:640], [pt.b], [btokS.b])
                chunk(g, wg, gn, xtokS[:], xtokS.b, btokS[:], btokS.b, xcS[:, 4, :], xcS[:, 5, :], xcS.b,
                      SEQ, True, True, ynTS[:], ynTS.b)()
                B.dma(ynT_d[g * 512:(g + 1) * 512, HALF:HALF + 128].rearrange("(c p) t -> p c t", p=128), ynTS[:],
                      reads=[ynTS.b], writes=[ynT_db])

    fw.barrier()
    if STAGE >= 3:
        ssd_phase()
    fw.barrier()

    def attn_phase():
      with contextlib.ExitStack() as ph0:
        onesb = B.sb(ph0, "onesb", [128, 128], BF16)
        qTSa = B.sb(ph0, "qTSa", [128, 12, 128], BF16)
        qTSb = B.sb(ph0, "qTSb", [128, 12, 128], BF16)
        hm = B.sb(ph0, "hm", [128, 4], F32)
        B.dma(hm[:], B.ins["c_hmask"], writes=[hm.b])
        nn_sb = B.sb(ph0, "nn_sb", [128, 4, 512], F32)
        gattS = B.sb(ph0, "gattS", [128, 4, 128], BF16)
        attS = B.sb(ph0, "attS", [128, 4, 128], BF16)
        with contextlib.ExitStack() as ph:
            attT = B.sb(ph, "attTl", [128, HALF], BF16)
            vnewS = B.sb(ph, "vnewS", [128, 3, 512], BF16)
            B.dma(vnewS[:].rearrange("p g n -> p (g n)"), vnew_d, reads=[vnew_db], writes=[vnewS.b])
            B.memset("pool", onesb[:], 1.0, [onesb.b])
            flg = B.sb(ph, "flgA", [128, 2], F32)
            B.dma(flg[:], flag, writes=[flg.b])
            wsls = [B.sb(ph, "wsl%d" % i, [128, 8, 1280], BF16) for i in range(1)]
            abts = [B.sb(ph, "abt%d" % i, [128, 3, 512], F32) for i in range(1)]
            sbts = [B.sb(ph, "sbt%d" % i, [128, 3, 256], F32) for i in range(1)]
            qTa = B.sb(ph, "qTa", [128, 3, HALF], BF16)
            qTb = B.sb(ph, "qTb", [128, 3, HALF], BF16)
            KOFF = (0, 2176, 4736)
            kT = B.sb(ph, "kT", [128, 8832], BF16)
            Vp = B.sb(ph, "Vp", [128, 69, 128], BF16)
            acc = B.sb(ph, "acc", [128, 2, HALF], F32)
            gatt = B.sb(ph, "gatt", [128, HALF], BF16)
            scs = [B.sb(ph, "sc%d" % i, [128, 512], F32) for i in range(2)]
            PTs = [B.sb(ph, "PT%d" % i, [128, 512], BF16) for i in range(2)]
            kTS = B.sb(ph, "kTS", [128, 12, 128], BF16)
            cnt = {"e": 0, "b": 0}

            def evac(out, in_, reads, writes, scale=None):
                e = "act" if cnt["e"] % 2 == 0 else "dve"
                cnt["e"] += 1
                if scale is None:
                    B.copy(e, out, in_, reads, writes)
                elif e == "act":
                    B.act(out, in_, AF.Copy, reads, writes, scale=scale)
                else:
                    B.ts("dve", out, in_, scale, None, ALU.mult, None, reads, writes)

            def dil_view(ap512, r):
                return ap512.rearrange("p (i r) -> p r i", r=r)

            for j in range(KSLOTS):
                wsl = wsls[0]
                abt = abts[0]
                sbt = sbts[0]
                for g in range(3):
                    for sec, off in enumerate((OFF_Q, OFF_K, OFF_V)):
                        c0 = (sec * 3 + g) * 128
                        s0 = off + g * 512 + j * 128
                        B.dma(wsl[:, :, c0:c0 + 128], w_in[:, s0:s0 + 128].rearrange("(k p) n -> p k n", p=128),
                              writes=[wsl.b], queue="pool")
                B.dma(wsl[:, :, 1152:1280], w_in[:, OFF_GATT + j * 128:OFF_GATT + (j + 1) * 128].rearrange("(k p) n -> p k n", p=128),
                      writes=[wsl.b], queue="pool")
                B.dma(abt[:].rearrange("p g n -> p (g n)"), B.ins["c_abias"][j], writes=[abt.b])
                B.dma(sbt[:].rearrange("p g n -> p (g n)"), B.ins["c_sbias"][j], writes=[sbt.b])
                B.memset("pool", acc[:], 0.0, [acc.b])

                def proj_T(col0, L0, n):
                    pt = B.ps()
                    B.mm_group(pt.t[:, 0:n], [(wsl[:, k, col0:col0 + 128], H(k, L0, n)) for k in range(8)],
                               [wsl.b, Hb(L0)], pt)
                    return pt

                for tb in range(4):
                    L0 = HALF + tb * 512
                    for g in range(3):
                        pt = proj_T(g * 128, L0, 512)
                        for qi_, qTx in enumerate((qTa, qTb)):
                            if g == 0:
                                dst = qTx[:, 0, tb * 512:(tb + 1) * 512]
                                src = pt.t[:]
                            elif g == 1:
                                dst = qTx[:, 1, tb * 512:(tb + 1) * 512].rearrange("p (r i) -> p r i", r=4)
                                src = dil_view(pt.t[:], 4)
                            else:
                                dst = qTx[:, 2, :].rearrange("p (r i) -> p r i", r=16)[:, :, tb * 32:(tb + 1) * 32]
                                src = dil_view(pt.t[:], 16)
                            if qi_ == 0:
                                B.act(dst, src, AF.Identity, [pt.b, hm.b], [qTx.b], scale=hm[:, 0:1])
                            else:
                                B.ts("dve", dst, src, hm[:, 1:2], None, ALU.mult, None, [pt.b, hm.b], [qTx.b])
                        pt = proj_T((3 + g) * 128, L0, 512)
                        if g == 0:
                            dst = kT[:, 128 + tb * 512:128 + (tb + 1) * 512]
                            src = pt.t[:]
                        elif g == 1:
                            o = KOFF[1] + (tb + 1) * 512
                            dst = kT[:, o:o + 512].rearrange("p (r i) -> p r i", r=4)
                            src = dil_view(pt.t[:], 4)
                        else:
                            o = KOFF[2] + 2048
                            dst = kT[:, o:o + 2048].rearrange("p (r i) -> p r i", r=16)[:, :, tb * 32:(tb + 1) * 32]
                            src = dil_view(pt.t[:], 16)
                        evac(dst, src, [pt.b], [kT.b])
                    pt = proj_T(1152, L0, 512)
                    B.act(gatt[:, tb * 512:(tb + 1) * 512], pt.t[:], AF.Silu, [pt.b], [gatt.b])
                pt = proj_T(3 * 128, HALF - 128, 128)
                evac(kT[:, 0:128], pt.t[:, 0:128], [pt.b], [kT.b])
                pt = proj_T(4 * 128, HALF - 512, 512)
                evac(kT[:, KOFF[1]:KOFF[1] + 512].rearrange("p (r i) -> p r i", r=4), dil_view(pt.t[:], 4), [pt.b], [kT.b])
                for tb in range(4):
                    pt = proj_T(5 * 128, tb * 512, 512)
                    evac(kT[:, KOFF[2]:KOFF[2] + 2048].rearrange("p (r i) -> p r i", r=16)[:, :, tb * 32:(tb + 1) * 32],
                         dil_view(pt.t[:], 16), [pt.b], [kT.b])
                for g in range(3):
                    pt = proj_T(g * 128, SEQ, 128)
                    B.act(qTSa[:, g * 4 + j, :], pt.t[:, 0:128], AF.Identity, [pt.b, hm.b], [qTSa.b], scale=hm[:, 0:1])
                    B.ts("dve", qTSb[:, g * 4 + j, :], pt.t[:, 0:128], hm[:, 1:2], None, ALU.mult, None, [pt.b, hm.b], [qTSb.b])
                    pt = proj_T((3 + g) * 128, SEQ, 128)
                    evac(kTS[:, g * 4 + j, :], pt.t[:, 0:128], [pt.b], [kTS.b])
                pt = proj_T(1152, SEQ, 128)
                B.act(gattS[:, j, :], pt.t[:, 0:128], AF.Silu, [pt.b], [gattS.b])
                vblocks = []
                for kb in range(17):
                    vblocks.append((0, HALF - 128 + kb * 128, 1))
                for b1 in range(-1, 4):
                    for r4 in range(4):
                        vblocks.append((1, HALF + b1 * 512 + r4, 4))
                for hb in range(2):
                    for r16 in range(16):
                        vblocks.append((2, hb * HALF + r16, 16))
                for q0 in range(0, 69, 4):
                    grp = vblocks[q0:q0 + 4]
                    pt = B.ps()
                    for bi, (g, La, step) in enumerate(grp):
                        last = (bi == len(grp) - 1)
                        for k in range(8):
                            B.mm(pt.t[:, bi * 128:(bi + 1) * 128], H(k, La, 128, step), wsl[:, k, (6 + g) * 128:(7 + g) * 128],
                                 (bi == 0 and k == 0), (last and k == 7), [Hb(La), wsl.b], [pt.b], (last and k == 7))
                    n = len(grp)
                    evac(Vp[:, q0:q0 + n, :], pt.t[:, 0:n * 128].rearrange("p (b c) -> p b c", c=128), [pt.b], [Vp.b])

                for g in range(3 if KSUB >= 2 else 0):
                    for qb in range(16):
                        if g == 0:
                            kown = KOFF[0] + (qb + 1) * 128
                            kprev = kown - 128
                            vown, vprev = 1 + qb, qb
                            halo = (qb == 0)
                            tok = lambda t, qb=qb: t[:, :, qb * 128:(qb + 1) * 128]
                        elif g == 1:
                            b1, r4 = qb // 4, qb % 4
                            kown = KOFF[1] + (b1 + 1) * 512 + r4 * 128
                            kprev = kown - 512
                            vown = 17 + (b1 + 1) * 4 + r4
                            vprev = vown - 4
                            halo = (b1 == 0)
                            tok = lambda t, b1=b1, r4=r4: t[:, :, b1 * 512 + r4:b1 * 512 + r4 + 509:4]
                        else:
                            kown = KOFF[2] + 2048 + qb * 128
                            kprev = kown - 2048
                            vown = 37 + 16 + qb
                            vprev = vown - 16
                            halo = True
                            tok = lambda t, qb=qb: t[:, :, qb:qb + 2033:16]
                        i = cnt["b"] % 2
                        cnt["b"] += 1
                        sc, PT = scs[i], PTs[i]
                        psc = B.ps()
                        n = 0
                        for kbi, kcol in enumerate((kprev, kown)):
                            for h in range(2):
                                qTx = qTa if h == 0 else qTb
                                B.mm(psc.t[:, (kbi * 2 + h) * 128:(kbi * 2 + h + 1) * 128], kT[:, kcol:kcol + 128],
                                     qTx[:, g, qb * 128:(qb + 1) * 128], n == 0, n == 3, [kT.b, qTx.b], [psc.b], n == 3)
                                n += 1
                        if halo:
                            B.stt("dve", sc[:, 0:256], psc.t[:, 0:256], flg[:, 1:2], abt[:, g, 0:256], ALU.add, ALU.add,
                                  [psc.b, flg.b, abt.b], [sc.b])
                            B.tt("dve", sc[:, 256:512], psc.t[:, 256:512], abt[:, g, 256:512], ALU.add, [psc.b, abt.b], [sc.b])
                        else:
                            B.tt("dve", sc[:], psc.t[:], abt[:, g, :], ALU.add, [psc.b, abt.b], [sc.b])
                        B.act(PT[:], sc[:], AF.Exp, [sc.b], [PT.b])
                        if KBLK < 2:
                            continue
                        ppv = B.ps()
                        for kbi, vb in enumerate((vprev, vown)):
                            B.mm(ppv.t[:, 0:256], Vp[:, vb, :], PT[:, kbi * 256:(kbi + 1) * 256], kbi == 0, kbi == 1,
                                 [Vp.b, PT.b], [ppv.b], False)
                        for kbi in range(2):
                            B.mm(ppv.t[:, 256:512], onesb[:], PT[:, kbi * 256:(kbi + 1) * 256], False, kbi == 1,
                                 [onesb.b, PT.b], [ppv.b], kbi == 1)
                        for h in range(2 if KBLK >= 3 else 0):
                            src = ppv.t[:].rearrange("p (nd hh q) -> p nd hh q", nd=2, hh=2)[:, :, h, :]
                            dst = tok(acc[:])
                            B.stt("dve", dst, src, hm[:, 2 + h:3 + h], dst, ALU.mult, ALU.add, [acc.b, ppv.b, hm.b], [acc.b])
                B.recip(acc[:, 1, :], acc[:, 1, :], [acc.b], [acc.b])
                B.tt("dve", acc[:, 0, :], acc[:, 0, :], acc[:, 1, :], ALU.mult, [acc.b], [acc.b])
                B.tt("dve", attT[:], acc[:, 0, :], gatt[:, 0:HALF], ALU.mult, [acc.b, gatt.b], [attT.b])
                B.dma(attT_d[j * 128:(j + 1) * 128, 0:HALF], attT[:], reads=[attT.b], writes=[attT_db])

                ppn = B.ps()
                B.hold(ppn)
                for g in range(3 if KSUB >= 3 else 0):
                    gc = g * 4 + j
                    i = cnt["b"] % 2
                    cnt["b"] += 1
                    sc, PT = scs[i], PTs[i]
                    psn = B.ps()
                    for h in range(2):
                        qTx = qTSa if h == 0 else qTSb
                        B.mm(psn.t[:, h * 128:(h + 1) * 128], kTS[:, gc, :], qTx[:, gc, :], h == 0, h == 1,
                             [kTS.b, qTx.b], [psn.b], h == 1)
                    B.tt("dve", sc[:, 0:256], psn.t[:, 0:256], sbt[:, g, :], ALU.add, [psn.b, sbt.b], [sc.b])
                    B.act(PT[:, 0:256], sc[:, 0:256], AF.Exp, [sc.b], [PT.b])
                    B.mm(ppn.t[:, 0:256], vnewS[:, g, j * 128:(j + 1) * 128], PT[:, 0:256], g == 0, g == 2,
                         [vnewS.b, PT.b], [ppn.b], False)
                    B.mm(ppn.t[:, 256:512], onesb[:], PT[:, 0:256], False, g == 2, [onesb.b, PT.b], [ppn.b], g == 2)
                B.copy("act", nn_sb[:, j, :], ppn.t[:], [ppn.b], [nn_sb.b])
                B.release(ppn)

        fw.barrier()
        with contextlib.ExitStack() as ph:
            cnt = {"e": 0}

            def evac(out, in_, reads, writes):
                e = "act" if cnt["e"] % 2 == 0 else "dve"
                cnt["e"] += 1
                B.copy(e, out, in_, reads, writes)
            cbias = B.sb(ph, "cbias", [128, 832], F32)
            B.dma(cbias[:], B.ins["c_cbias"], writes=[cbias.b])
            QBD = B.sb(ph, "QBD", [128, 12, 16, 16], BF16)
            B.copy("dve", QBD[:, :, :, 0:8], qTSa[:].rearrange("p g (b s) -> p g b s", s=8), [qTSa.b], [QBD.b])
            B.copy("dve", QBD[:, :, :, 8:16], qTSb[:].rearrange("p g (b s) -> p g b s", s=8), [qTSb.b], [QBD.b])
            kvbs = [B.sb(ph, "kvb%d" % i, [128, 13, 1024], BF16) for i in range(2)]
            KTs = [B.sb(ph, "KTb%d" % i, [128, 4, 13, 128], BF16) for i in range(2)]
            scS = [B.sb(ph, "scS%d" % i, [128, 832], F32) for i in range(2)]
            PTS = [B.sb(ph, "PTS%d" % i, [128, 832], BF16) for i in range(2)]
            nd_sb = B.sb(ph, "nd_sb", [128, 16, 128], F32)
            for b in range(NB_S if KSUB >= 4 else 0):
                kvb, KT, sc, PT = kvbs[b % 2], KTs[b % 2], scS[b % 2], PTS[b % 2]
                B.dma(kvb[:, 0, :], kc[0][b], writes=[kvb.b], queue="pool")
                B.dma(kvb[:, 1:5, :], kc[1][b].rearrange("(p r) n -> p r n", r=4), writes=[kvb.b], queue="pool")
                B.dma(kvb[:, 5:13, :], kc[2][b].rearrange("(p r) n -> p r n", r=16)[:, 0:8, :], writes=[kvb.b], queue="pool")
                for c in range(4):
                    for (t0, nt) in ((0, 8), (8, 5)):
                        pt = B.ps()
                        ptb = pt.t[:].bitcast(BF16)
                        for ti in range(nt):
                            B.transpose(ptb[:, ti * 128:(ti + 1) * 128], kvb[:, t0 + ti, c * 128:(c + 1) * 128], ident_b[:],
                                        [kvb.b, ident_b.b], pt, inc=(ti == nt - 1))
                        evac(KT[:, c, t0:t0 + nt, :], ptb[:, 0:nt * 128].rearrange("p (t q) -> p t q", q=128), [pt.b], [KT.b])
                for (t0, nt) in ((0, 8), (8, 5)):
                    pss = B.ps()
                    n = 0
                    for ti in range(nt):
                        t = t0 + ti
                        g = 0 if t == 0 else (1 if t < 5 else 2)
                        for c in range(4):
                            B.mm(pss.t[:, ti * 64 + c * 16:ti * 64 + (c + 1) * 16], KT[:, c, t, :], QBD[:, g * 4 + c, b, :],
                                 n == 0, n == nt * 4 - 1, [KT.b, QBD.b], [pss.b], n == nt * 4 - 1)
                            n += 1
                    B.tt("dve", sc[:, t0 * 64:(t0 + nt) * 64], pss.t[:, 0:nt * 64], cbias[:, t0 * 64:(t0 + nt) * 64], ALU.add,
                         [pss.b, cbias.b], [sc.b])
                B.act(PT[:], sc[:], AF.Exp, [sc.b], [PT.b])
                ppv = B.ps()
                n = 0
                for c in range(4):
                    for t in range(13):
                        B.mm(ppv.t[:, c * 16:(c + 1) * 16], kvb[:, t, 512 + c * 128:512 + (c + 1) * 128],
                             PT[:, t * 64 + c * 16:t * 64 + (c + 1) * 16], n == 0, False, [kvb.b, PT.b], [ppv.b], False)
                        n += 1
                for t in range(13):
                    B.mm(ppv.t[:, 64:128], onesb[:], PT[:, t * 64:(t + 1) * 64], False, t == 12, [onesb.b, PT.b], [ppv.b], t == 12)
                B.copy("act", nd_sb[:, b, :], ppv.t[:, 0:128], [ppv.b], [nd_sb.b])
            numS = B.sb(ph, "numS", [128, 128], F32)
            denS = B.sb(ph, "denS", [128, 128], F32)
            tmpA = B.sb(ph, "tmpA", [128, 128], F32)
            tmpB = B.sb(ph, "tmpB", [128, 128], F32)
            v = lambda ap: ap.rearrange("p (b s) -> p b s", s=8)
            for c in range(4):
                for (dstT, ndoff, nnoff) in ((numS, c * 16, 0), (denS, 64 + 2 * c * 8, 256)):
                    B.tt("dve", v(tmpA[:]), nd_sb[:, :, ndoff:ndoff + 8], v(nn_sb[:, c, nnoff:nnoff + 128]), ALU.add,
                         [nd_sb.b, nn_sb.b], [tmpA.b])
                    B.tt("dve", v(tmpB[:]), nd_sb[:, :, ndoff + 8:ndoff + 16], v(nn_sb[:, c, nnoff + 128:nnoff + 256]), ALU.add,
                         [nd_sb.b, nn_sb.b], [tmpB.b])
                    B.ts("dve", dstT[:], tmpA[:], hm[:, 2:3], None, ALU.mult, None, [tmpA.b, hm.b], [dstT.b])
                    B.stt("dve", dstT[:], tmpB[:], hm[:, 3:4], dstT[:], ALU.mult, ALU.add, [tmpB.b, hm.b, dstT.b], [dstT.b])
                B.recip(denS[:], denS[:], [denS.b], [denS.b])
                B.tt("dve", numS[:], numS[:], denS[:], ALU.mult, [numS.b, denS.b], [numS.b])
                B.tt("dve", attS[:, c, :], numS[:], gattS[:, c, :], ALU.mult, [numS.b, gattS.b], [attS.b])
            B.dma(attT_d[:, HALF:HALF + 128].rearrange("(c p) t -> p c t", p=128), attS[:], reads=[attS.b], writes=[attT_db])

    if STAGE >= 4:
        attn_phase()
    fw.barrier()

    halo_scope.close()

    def back_phase():
        with contextlib.ExitStack() as ph:
            watt = B.sb(ph, "watt", [128, 4, D], BF16)
            B.dma(watt[:], w_att.rearrange("(k p) n -> p k n", p=128), writes=[watt.b], queue="pool")
            wssm = B.sb(ph, "wssm", [128, 16, D], BF16)
            for q in range(4):
                B.dma(wssm[:, q * 4:(q + 1) * 4, :], w_ssm[q * 512:(q + 1) * 512, :].rearrange("(k p) n -> p k n", p=128),
                      writes=[wssm.b], queue="pool")
            wout = B.sb(ph, "wout", [128, 8, D], BF16)
            for q in range(2):
                B.dma(wout[:, q * 4:(q + 1) * 4, :], w_out[q * 512:(q + 1) * 512, :].rearrange("(k p) n -> p k n", p=128),
                      writes=[wout.b], queue="pool")
            gateP = B.sb(ph, "gateP", [128, D], F32)
            gateS = B.sb(ph, "gateS", [128, D], F32)
            B.dma(gateP[:], gate_d[0], reads=[gate_db], writes=[gateP.b])
            B.dma(gateS[:], gate_d[1], reads=[gate_db], writes=[gateS.b])
            attb = B.sb(ph, "attb", [128, 4, 512], BF16)
            fng = B.sb(ph, "fng", [128, D], F32)
            B.dma(fng[:], fnorm_g[0:1, :].partition_broadcast(128), writes=[fng.b])
            wgab = B.sb(ph, "wgab", [128, 8, 2048], BF16)
            for q in range(4):
                B.dma(wgab[:, :, q * 512:(q + 1) * 512], w_in[:, OFF_GA + q * 512:OFF_GA + (q + 1) * 512].rearrange("(k p) n -> p k n", p=128),
                      writes=[wgab.b], queue="pool")
            ynb = B.sb(ph, "ynblk", [128, 16, 512], BF16)
            sga = [B.sb(ph, "sga%d" % i, [128, 512], F32) for i in range(2)]
            sgb = [B.sb(ph, "sgb%d" % i, [128, 512], F32) for i in range(2)]
            ta = [B.sb(ph, "ta%d" % i, [128, 512], F32) for i in range(2)]
            tb_ = [B.sb(ph, "tb%d" % i, [128, 512], F32) for i in range(2)]
            mg = B.sb(ph, "mg", [128, 8, 512], BF16)
            xin = [B.sb(ph, "xres%d" % i, [128, D], F32) for i in range(2)]
            ob = [B.sb(ph, "ob%d" % i, [128, D], F32) for i in range(2)]
            jk = B.sb(ph, "jkb", [128, D], F32)
            stt_ = [B.sb(ph, "stb%d" % i, [128, 4], F32) for i in range(2)]
            it = 0
            for blk in range(5):
                n = 512 if blk < 4 else 128
                t0 = blk * 512
                for q in range(4):
                    B.dma(ynb[:, q * 4:(q + 1) * 4, 0:n], ynT_d[q * 512:(q + 1) * 512, t0:t0 + n].rearrange("(c p) t -> p c t", p=128),
                          reads=[ynT_db], writes=[ynb.b])
                B.dma(attb[:, :, 0:n], attT_d[:, t0:t0 + n].rearrange("(c p) t -> p c t", p=128), reads=[attT_db], writes=[attb.b])
                for dc in range(8):
                    w = wgab
                    i = dc % 2
                    pga = B.ps()
                    B.mm_group(pga.t[:, 0:n], [(w[:, k, dc * 128:(dc + 1) * 128], hTo[:, k, t0:t0 + n]) for k in range(8)], [w.b, hTo.b], pga)
                    B.act(sga[i][:, 0:n], pga.t[:, 0:n], AF.Sigmoid, [pga.b], [sga[i].b])
                    pgb = B.ps()
                    B.mm_group(pgb.t[:, 0:n], [(w[:, k, 1024 + dc * 128:1024 + (dc + 1) * 128], hTo[:, k, t0:t0 + n]) for k in range(8)], [w.b, hTo.b], pgb)
                    B.act(sgb[i][:, 0:n], pgb.t[:, 0:n], AF.Sigmoid, [pgb.b], [sgb[i].b])
                    pa = B.ps()
                    B.mm_group(pa.t[:, 0:n], [(watt[:, k, dc * 128:(dc + 1) * 128], attb[:, k, 0:n]) for k in range(4)],
                               [watt.b, attb.b], pa)
                    B.tt("dve", ta[i][:, 0:n], pa.t[:, 0:n], sga[i][:, 0:n], ALU.mult, [pa.b, sga[i].b], [ta[i].b])
                    pm = B.ps()
                    B.mm_group(pm.t[:, 0:n], [(wssm[:, k, dc * 128:(dc + 1) * 128], ynb[:, k, 0:n]) for k in range(16)],
                               [wssm.b, ynb.b], pm)
                    B.tt("dve", tb_[i][:, 0:n], pm.t[:, 0:n], sgb[i][:, 0:n], ALU.mult, [pm.b, sgb[i].b], [tb_[i].b])
                    B.tt("dve", mg[:, dc, 0:n], ta[i][:, 0:n], tb_[i][:, 0:n], ALU.add, [ta[i].b, tb_[i].b], [mg.b])
                for tl in range(n // 128):
                    xi = xin[it % 2]
                    o = ob[it % 2]
                    sT = stt_[it % 2]
                    it += 1
                    if blk < 4:
                        row = blk * 512 + tl * 128
                        B.dma(xi[:], xp[HALF + row:HALF + row + 128, :], writes=[xi.b])
                        gate, dst = gateP, yp[row:row + 128, :]
                    else:
                        B.dma(xi[:], xs, writes=[xi.b])
                        gate, dst = gateS, ys
                    for half in range(2):
                        po = B.ps()
                        B.mm_group(po.t[:], [(mg[:, k, tl * 128:(tl + 1) * 128], wout[:, k, half * 512:(half + 1) * 512])
                                             for k in range(8)], [mg.b, wout.b], po)
                        B.tt("dve", o[:, half * 512:(half + 1) * 512], po.t[:], gate[:, half * 512:(half + 1) * 512], ALU.mult,
                             [po.b, gate.b], [o.b])
                    B.tt("dve", o[:], o[:], xi[:], ALU.add, [o.b, xi.b], [o.b])
                    B.act(jk[:], o[:], AF.Square, [o.b], [jk.b, sT.b], accum_out=sT[:, 0:1])
                    B.act(sT[:, 1:2], sT[:, 0:1], AF.Ln, [sT.b], [sT.b], bias=EPS, scale=1.0 / D)
                    B.act(sT[:, 2:3], sT[:, 1:2], AF.Exp, [sT.b], [sT.b], scale=-0.5)
                    B.stt("dve", o[:], o[:], sT[:, 2:3], fng[:], ALU.mult, ALU.mult, [o.b, sT.b, fng.b], [o.b])
                    B.dma(dst, o[:], reads=[o.b])

    fw.barrier()
    if STAGE >= 5:
        back_phase()
    else:
        with contextlib.ExitStack() as ph:
            z = B.sb(ph, "zeros", [128, 1024], F32)
            B.memset("dve", z[:], 0.0, [z.b])
            for i in range(16):
                B.dma(yp[i * 128:(i + 1) * 128, :], z[:], reads=[z.b])
                if STAGE < 3:
                    B.dma(ssmp[i * 128:(i + 1) * 128, :], z[:, 0:128], reads=[z.b])
            B.dma(ys, z[:], reads=[z.b])
            if STAGE < 3:
                for b in range(NB_S):
                    for hh in range(2):
                        B.dma(ssms[b, hh * 1024:(hh + 1) * 1024].rearrange("(a p) n -> p a n", p=128),
                              z[:].rearrange("p (a n) -> p a n", a=8), reads=[z.b])

    fw.finish()
    fw.emit()
    root.close()
    return B


def _host_consts():
    NEGM = -30000.0
    idx = np.arange(128)
    s_, l_ = idx[:, None], idx[None, :]
    tri = (s_ <= l_).astype(np.float32)
    same = ((s_ // 8) == (l_ // 8)).astype(np.float32)
    triS = tri * same
    mb = np.where(l_ >= s_, 0.0, NEGM).astype(np.float32)
    mbS = np.where((l_ >= s_) & (same > 0), 0.0, NEGM).astype(np.float32)
    selA = np.zeros((8, 8, 128), np.float32)
    selB = np.zeros((8, 2, 4, 128), np.float32)
    for h in range(8):
        selA[h, h, :] = 1.0
        selB[h, h // 4, h % 4, :] = 1.0
    bmask = np.zeros((128, 16, 128), np.float32)
    for b in range(16):
        bmask[:, b, b * 8:(b + 1) * 8] = 1.0
    rowmask = np.zeros((128, 16), np.float32)
    rowmask[idx, idx // 8] = 1.0
    n = 24
    slopes = (2.0 ** (-8.0 * np.arange(1, n + 1) / n)).reshape(3, 8)
    abias = np.full((4, 128, 3, 2, 2, 128), NEGM, np.float64)
    kj, qi = idx[:, None], idx[None, :]
    for g in range(3):
        dil = DILS[g]
        for kbi in range(2):
            delta = qi + 128 - kj if kbi == 0 else qi - kj
            valid = (delta >= 0) & (delta <= 128)
            for j in range(4):
                for h in range(2):
                    sl = slopes[g, 2 * j + h]
                    abias[j, :, g, kbi, h, :] = np.where(valid, -sl * delta * dil, NEGM)
    sbias = np.full((4, 128, 3, 2, 128), NEGM, np.float64)
    kb_, ks_ = idx[:, None] // 8, idx[:, None] % 8
    qb_, qs_ = idx[None, :] // 8, idx[None, :] % 8
    for g in range(3):
        dil = DILS[g]
        d = qs_ - ks_
        valid = (kb_ == qb_) & (d >= 0) & (d % dil == 0)
        for j in range(4):
            for h in range(2):
                sl = slopes[g, 2 * j + h]
                sbias[j, :, g, h, :] = np.where(valid, -sl * d, NEGM)
    cbias = np.full((128, 13, 8, 8), NEGM, np.float64)
    p_ = idx
    for h in range(8):
        for s in range(8):
            row = p_
            cbias[:, 0, h, s] = np.where(row >= s, -slopes[0, h] * (128 + s - row), NEGM)
            r = s % 4
            row = r + 4 * p_
            cbias[:, 1 + r, h, s] = np.where(row >= s, -slopes[1, h] * (512 + s - row), NEGM)
            row = s + 16 * p_
            cbias[:, 5 + s, h, s] = -slopes[2, h] * (2048 + s - row)
    f32 = lambda a: np.ascontiguousarray(a, dtype=np.float32)
    return {
        "c_tri": f32(tri), "c_triS": f32(triS), "c_blk": f32(same), "c_mb4": f32(np.tile(mb, (1, 4))),
        "c_mb4S": f32(np.tile(mbS, (1, 4))), "c_selA": f32(selA.reshape(8, 1024)), "c_selB": f32(selB.reshape(8, 1024)),
        "c_bmask": f32(bmask.reshape(128, 2048)), "c_rowmask": f32(rowmask),
        "c_abias": f32(abias.reshape(4, 128, 1536)), "c_sbias": f32(sbias.reshape(4, 128, 768)),
        "c_cbias": f32(cbias.reshape(128, 832)),
        "c_hmask": f32(np.stack([np.where(idx < 64, 0.125, 0.0), np.where(idx >= 64, 0.125, 0.0),
                                 np.where(idx < 64, 1.0, 0.0), np.where(idx >= 64, 1.0, 0.0)], axis=1)),
    }


_PROG = None


def _get_prog():
    global _PROG
    if _PROG is None:
        _PROG = build_program()
    return _PROG


def kernel(x_prompt, x_sample, c_prompt, c_sample, cache_kv_w128, cache_kv_w512, cache_kv_w2048,
           state_ssm, state_conv, norm_g, w_ada, b_ada, w_in, conv_w, conv_b, dt_bias, a_log,
           d_skip, ssm_norm_g, w_att_branch, w_ssm_branch, w_out, final_norm_g):
    f = lambda a: np.ascontiguousarray(np.asarray(a, dtype=np.float32))
    B = _get_prog()
    caches = (f(cache_kv_w128), f(cache_kv_w512), f(cache_kv_w2048))
    x_prompt = f(x_prompt)
    shared = {
        "w_ada": f(w_ada)[0], "b_ada": f(b_ada)[0][None, :], "w_in": f(w_in)[0], "conv_w": f(conv_w)[0],
        "conv_b": f(conv_b)[0][None, :], "dt_bias": f(dt_bias)[0][None, :], "a_log": f(a_log)[0][None, :],
        "d_skip": f(d_skip)[0][None, :], "ssm_norm_g": f(ssm_norm_g)[0][None, :], "w_att": f(w_att_branch)[0],
        "w_ssm": f(w_ssm_branch)[0], "w_out": f(w_out)[0], "norm_g": f(norm_g)[0][None, :],
        "fnorm_g": f(final_norm_g)[None, :], "ident": np.eye(128, dtype=np.float32),
    }
    shared.update(_host_consts())
    in_maps = []
    for c in range(8):
        b, hf = c // 2, c % 2
        if hf == 1:
            xpc = x_prompt[b]
        else:
            xpc = np.concatenate([np.zeros((HALF, D), np.float32), x_prompt[b, :HALF]], axis=0)
        sl = slice(c * NB_S, (c + 1) * NB_S)
        flagc = np.zeros((128, 2), np.float32)
        flagc[:, 0] = float(hf)
        flagc[:, 1] = 0.0 if hf == 1 else -30000.0
        m = dict(shared)
        m.update({
            "xp": np.ascontiguousarray(xpc),
            "xs": np.ascontiguousarray(f(x_sample)[sl].reshape(128, D)),
            "cp": np.ascontiguousarray(np.broadcast_to(f(c_prompt)[b][None, :], (128, D))),
            "cs": np.ascontiguousarray(np.repeat(f(c_sample)[sl], 8, axis=0)),
            "flag": flagc,
            "sst": np.ascontiguousarray(f(state_ssm)[0, sl].reshape(NB_S, 2048, 128)),
            "scv": np.ascontiguousarray(f(state_conv)[0, sl].reshape(NB_S * 3, 3072)),
        })
        for g in range(3):
            m["kc%d" % g] = np.ascontiguousarray(caches[g][0, sl].reshape(NB_S, WINS[g], 1024))
        m = {k: v for k, v in m.items() if k in B.ins}
        in_maps.append(m)
    res = run_bass_kernel_spmd(B.nc, in_maps, core_ids=list(range(8)))
    R = res.results
    y_prompt = np.stack([np.concatenate([R[2 * b]["yp"], R[2 * b + 1]["yp"]], axis=0) for b in range(4)])
    y_sample = np.concatenate([R[c]["ys"].reshape(NB_S, 8, D) for c in range(8)], axis=0)
    kvp = [np.stack([R[2 * b + 1]["kvp%d" % g].reshape(WINS[g], 2, 8, 64) for b in range(4)])[None] for g in range(3)]
    ssm_p = np.stack([R[2 * b + 1]["ssmp"].reshape(32, 64, 128) for b in range(4)])[None]
    conv_p = np.stack([R[2 * b + 1]["convp"] for b in range(4)])[None]
    kvs = [np.concatenate([R[c]["kvs%d" % g].reshape(NB_S, WINS[g], 2, 8, 64) for c in range(8)], axis=0)[None]
           for g in range(3)]
    ssm_s = np.concatenate([R[c]["ssms"].reshape(NB_S, 32, 64, 128) for c in range(8)], axis=0)[None]
    conv_s = np.concatenate([R[c]["convs"].reshape(NB_S, 3, 3072) for c in range(8)], axis=0)[None]
    return (y_prompt, y_sample, kvp[0], kvp[1], kvp[2], ssm_p, conv_p, kvs[0], kvs[1], kvs[2], ssm_s, conv_s)
```

```python
import contextlib
import numpy as np
import concourse.bass as bass
import concourse.mybir as mybir
from concourse.bass_utils import run_bass_kernel_spmd

F32 = mybir.dt.float32
BF16 = mybir.dt.bfloat16
AF = mybir.ActivationFunctionType
ALU = mybir.AluOpType
AX = mybir.AxisListType

COMPUTE = ("pe", "act", "dve", "pool")
import os as _os
STAGE = int(_os.environ.get("KSTAGE", "99"))
KSUB = int(_os.environ.get("KSUB", "99"))
KSLOTS = int(_os.environ.get("KSLOTS", "4"))
KBLK = int(_os.environ.get("KBLK", "99"))

D = 1024
SEQ = 4096
HALF = 2048
NB_S = 16
IN_W = 12320
OFF_Q, OFF_K, OFF_V, OFF_GATT, OFF_Z, OFF_XBC, OFF_DT, OFF_GA, OFF_GB = 0, 1536, 3072, 4608, 5120, 7168, 10240, 10272, 11296
WINS = (128, 512, 2048)
DILS = (1, 4, 16)
EPS = 1e-6
NEG = -1e30


class Buf:
    __slots__ = ("name", "writer", "readers", "excl")

    def __init__(self, name="", excl=False):
        self.name = name
        self.writer = None
        self.readers = []
        self.excl = excl


class FW:
    def __init__(self, nc, n_dma_sems=48):
        self.nc = nc
        self.streams = {k: [] for k in ("pe", "act", "dve", "pool", "sp")}
        self.sem = {k: nc.alloc_semaphore("sem_" + k) for k in COMPUTE}
        self.count = {k: 0 for k in COMPUTE}
        self.waited = {k: {} for k in self.streams}
        self.dma_sems = [nc.alloc_semaphore("dsem%d" % i) for i in range(n_dma_sems)]
        self.dma_val = [0] * n_dma_sems
        self.dma_pool = {"sp": list(range(0, 28)), "pool": list(range(28, n_dma_sems)), "act": []}
        self.dma_rr = {"sp": 0, "pool": 0}
        self.bg_sem = nc.alloc_semaphore("bgsem")
        self.bg_val = 0
        self.semobj = {k: self.sem[k] for k in COMPUTE}
        for i, s in enumerate(self.dma_sems):
            self.semobj[("d", i)] = s
        self.semobj["bg"] = self.bg_sem

    def _need(self, reads, writes):
        need = {}

        def add(kv):
            if kv is None:
                return
            k, v = kv
            if need.get(k, -1) < v:
                need[k] = v
        for b in reads:
            add(b.writer)
            if b.excl:
                for r in b.readers:
                    add(r)
        for b in writes:
            add(b.writer)
            for r in b.readers:
                add(r)
        return need

    def _emit_waits(self, eng, need, skip_self=False):
        waits = []
        w = self.waited[eng]
        for k, v in need.items():
            if skip_self and k == eng:
                continue
            if w.get(k, -1) >= v:
                continue
            w[k] = v
            waits.append((self.semobj[k], v))
        return waits

    def _update(self, reads, writes, tag):
        for b in reads:
            b.readers.append(tag)
            if len(b.readers) > 64:
                best = {}
                for k, v in b.readers:
                    if best.get(k, -1) < v:
                        best[k] = v
                b.readers = list(best.items())
        for b in writes:
            b.writer = tag
            b.readers = []

    def op(self, eng, fn, reads=(), writes=(), inc=True):
        need = self._need(reads, writes)
        waits = self._emit_waits(eng, need, skip_self=(eng == "pe"))
        if inc:
            self.count[eng] += 1
            tag = (eng, self.count[eng])
        else:
            tag = (eng, self.count[eng] + 1)
        self.streams[eng].append((waits, fn, (self.sem[eng], 1) if inc else None))
        self._update(reads, writes, tag)

    def dma(self, fn, reads=(), writes=(), queue="sp"):
        need = self._need(reads, writes)
        pool = self.dma_pool[queue]
        i = pool[self.dma_rr[queue] % len(pool)]
        self.dma_rr[queue] += 1
        key = ("d", i)
        if self.dma_val[i] > 0 and need.get(key, -1) < self.dma_val[i]:
            need[key] = self.dma_val[i]
        waits = self._emit_waits(queue, need)
        self.dma_val[i] += 16
        tag = (key, self.dma_val[i])
        self.streams[queue].append((waits, fn, (self.dma_sems[i], 16)))
        self._update(reads, writes, tag)

    def dma_bg(self, fn, queue="act"):
        self.bg_val += 16
        self.streams[queue].append(([], fn, (self.bg_sem, 16)))

    def barrier(self):
        need = {}
        for k in COMPUTE:
            if self.count[k] > 0:
                need[k] = self.count[k]
        for i, v in enumerate(self.dma_val):
            if v > 0:
                need[("d", i)] = v
        for eng in self.streams:
            waits = self._emit_waits(eng, dict(need), skip_self=(eng == "pe"))
            if waits:
                self.streams[eng].append((waits, None, None))

    def finish(self, eng="sp"):
        need = {}
        for k in COMPUTE:
            if self.count[k] > 0:
                need[k] = self.count[k]
        for i, v in enumerate(self.dma_val):
            if v > 0:
                need[("d", i)] = v
        if self.bg_val > 0:
            need["bg"] = self.bg_val
        waits = self._emit_waits(eng, need)
        self.streams[eng].append((waits, None, None))

    def emit(self):
        nc = self.nc
        streams = self.streams

        def run(engobj, lst):
            for waits, fn, inc in lst:
                for s, v in waits:
                    engobj.wait_ge(s, v)
                if fn is None:
                    continue
                ins = fn(engobj)
                if inc is not None:
                    ins.then_inc(inc[0], inc[1])

        with nc.Block() as block:
            @block.sync
            def _(e):
                run(e, streams["sp"])

            @block.tensor
            def _(e):
                run(e, streams["pe"])

            @block.scalar
            def _(e):
                run(e, streams["act"])

            @block.vector
            def _(e):
                run(e, streams["dve"])

            @block.gpsimd
            def _(e):
                run(e, streams["pool"])


class Tile:
    def __init__(self, t, name):
        self.t = t
        self.b = Buf(name)

    def __getitem__(self, k):
        return self.t[k]


class Builder:
    def __init__(self):
        self.nc = bass.Bass("TRN2", target_bir_lowering=False)
        self.fw = FW(self.nc)
        self.root = contextlib.ExitStack()
        self.ins = {}
        self.outs = {}
        self.uid = 0
        self.psum_tiles = []
        self.psum_rr = 0
        self.held = set()

    def din(self, name, shape):
        self.ins[name] = self.nc.dram_tensor(name, list(shape), F32, kind="ExternalInput").ap()
        return self.ins[name]

    def dout(self, name, shape):
        self.outs[name] = self.nc.dram_tensor(name, list(shape), F32, kind="ExternalOutput").ap()
        return self.outs[name]

    def sb(self, stack, name, shape, dt=F32):
        self.uid += 1
        t = stack.enter_context(self.nc.sbuf_tensor("%s_%d" % (name, self.uid), list(shape), dt))
        return Tile(t, name)

    def init_psum(self):
        for i in range(8):
            t = self.root.enter_context(self.nc.psum_tensor("psb%d" % i, [128, 512], F32))
            tl = Tile(t, "ps%d" % i)
            tl.b.excl = True
            self.psum_tiles.append(tl)

    def ps(self):
        while True:
            t = self.psum_tiles[self.psum_rr]
            self.psum_rr = (self.psum_rr + 1) % 8
            if id(t) not in self.held:
                return t

    def hold(self, t):
        self.held.add(id(t))

    def release(self, t):
        self.held.discard(id(t))

    def dma(self, out, in_, reads=(), writes=(), queue="sp"):
        self.fw.dma(lambda e: e.dma_start(out=out, in_=in_), reads=reads, writes=writes, queue=queue)

    def mm(self, out, lhsT, rhs, start, stop, reads, writes, inc):
        self.fw.op("pe", lambda e: e.matmul(out, lhsT=lhsT, rhs=rhs, start=start, stop=stop, skip_group_check=True),
                   reads=reads, writes=writes, inc=inc)

    def mm_group(self, out, pairs, reads, pst, last_inc=True):
        n = len(pairs)
        for i, (l, r) in enumerate(pairs):
            self.mm(out, l, r, i == 0, i == n - 1, reads, [pst.b], inc=(last_inc and i == n - 1))

    def transpose(self, out, in_, ident, reads, pst, inc=True):
        self.fw.op("pe", lambda e: e.transpose(out, in_, ident), reads=reads, writes=[pst.b], inc=inc)

    def act(self, out, in_, func, reads, writes, bias=None, scale=None, accum_out=None):
        kw = {}
        if bias is not None:
            kw["bias"] = bias
        if scale is not None:
            kw["scale"] = scale
        if accum_out is not None:
            kw["accum_out"] = accum_out
        self.fw.op("act", lambda e: e.activation(out=out, in_=in_, func=func, **kw), reads=reads, writes=writes)

    def tt(self, eng, out, in0, in1, op, reads, writes):
        self.fw.op(eng, lambda e: e.tensor_tensor(out=out, in0=in0, in1=in1, op=op), reads=reads, writes=writes)

    def ts(self, eng, out, in0, s1, s2, op0, op1, reads, writes):
        if s2 is None:
            self.fw.op(eng, lambda e: e.tensor_scalar(out=out, in0=in0, scalar1=s1, scalar2=None, op0=op0),
                       reads=reads, writes=writes)
        else:
            self.fw.op(eng, lambda e: e.tensor_scalar(out=out, in0=in0, scalar1=s1, scalar2=s2, op0=op0, op1=op1),
                       reads=reads, writes=writes)

    def stt(self, eng, out, in0, scalar, in1, op0, op1, reads, writes):
        self.fw.op(eng, lambda e: e.scalar_tensor_tensor(out=out, in0=in0, scalar=scalar, in1=in1, op0=op0, op1=op1),
                   reads=reads, writes=writes)

    def copy(self, eng, out, in_, reads, writes):
        if eng == "act":
            self.fw.op("act", lambda e: e.copy(out=out, in_=in_), reads=reads, writes=writes)
        else:
            self.fw.op(eng, lambda e: e.tensor_copy(out=out, in_=in_), reads=reads, writes=writes)

    def memset(self, eng, ap, val, writes):
        self.fw.op(eng, lambda e: e.memset(ap, val), writes=writes)

    def recip(self, out, in_, reads, writes):
        self.fw.op("dve", lambda e: e.reciprocal(out=out, in_=in_), reads=reads, writes=writes)

    def reduce(self, out, in_, op, reads, writes):
        self.fw.op("dve", lambda e: e.tensor_reduce(out=out, in_=in_, axis=AX.X, op=op), reads=reads, writes=writes)


def build_program():
    B = Builder()
    nc, fw = B.nc, B.fw
    B.init_psum()
    root = B.root

    xp = B.din("xp", [SEQ, D])
    xs = B.din("xs", [128, D])
    cp = B.din("cp", [128, D])
    cs = B.din("cs", [128, D])
    flag = B.din("flag", [128, 2])
    kc = [B.din("kc%d" % g, [NB_S, WINS[g], 1024]) for g in range(3)]
    sst = B.din("sst", [NB_S, 2048, 128])
    scv = B.din("scv", [NB_S * 3, 3072])
    w_ada = B.din("w_ada", [D, 3 * D])
    b_ada = B.din("b_ada", [1, 3 * D])
    w_in = B.din("w_in", [D, IN_W])
    conv_w = B.din("conv_w", [4, 3072])
    conv_b = B.din("conv_b", [1, 3072])
    dt_bias = B.din("dt_bias", [1, 32])
    a_log = B.din("a_log", [1, 32])
    d_skip = B.din("d_skip", [1, 32])
    ssm_norm_g = B.din("ssm_norm_g", [1, 2048])
    w_att = B.din("w_att", [512, D])
    w_ssm = B.din("w_ssm", [2048, D])
    w_out = B.din("w_out", [D, D])
    norm_g = B.din("norm_g", [1, D])
    fnorm_g = B.din("fnorm_g", [1, D])
    ident_d = B.din("ident", [128, 128])
    for nm, shp in (("c_tri", [128, 128]), ("c_triS", [128, 128]), ("c_blk", [128, 128]), ("c_mb4", [128, 512]),
                    ("c_mb4S", [128, 512]), ("c_selA", [8, 1024]), ("c_selB", [8, 1024]), ("c_bmask", [128, 2048]),
                    ("c_rowmask", [128, 16]), ("c_abias", [4, 128, 1536]), ("c_sbias", [4, 128, 768]),
                    ("c_cbias", [128, 832]), ("c_hmask", [128, 4])):
        B.din(nm, shp)

    yp = B.dout("yp", [HALF, D])
    ys = B.dout("ys", [128, D])
    kvp = [B.dout("kvp%d" % g, [WINS[g], 1024]) for g in range(3)]
    ssmp = B.dout("ssmp", [2048, 128])
    convp = B.dout("convp", [3, 3072])
    kvs = [B.dout("kvs%d" % g, [NB_S, WINS[g], 1024]) for g in range(3)]
    ssms = B.dout("ssms", [NB_S, 2048, 128])
    convs = B.dout("convs", [NB_S * 3, 3072])

    def issue_cache_shift():
        for g in range(3):
            Lc = WINS[g]
            for b in range(NB_S):
                o = kvs[g][b, 0:Lc - 8, :].rearrange("(a r) n -> a (r n)", a=8)
                i = kc[g][b, 8:Lc, :].rearrange("(a r) n -> a (r n)", a=8)
                fw.dma_bg((lambda o, i: (lambda e: e.dma_start(out=o, in_=i)))(o, i), queue="sp")

    ident_f = B.sb(root, "ident_f", [128, 128], F32)
    ident_b = B.sb(root, "ident_b", [128, 128], BF16)
    hTo = B.sb(root, "hTo", [128, 8, HALF + 128], BF16)
    halo_scope = contextlib.ExitStack()
    hTh = B.sb(halo_scope, "hTh", [128, 8, HALF], BF16)

    def H(k, a, n, step=1):
        t, o = (hTh, a) if a < HALF else (hTo, a - HALF)
        if step == 1:
            return t[:, k, o:o + n]
        return t[:, k, o:o + (n - 1) * step + 1:step]

    def Hb(a):
        return hTh.b if a < HALF else hTo.b

    def Hall(a, n):
        t, o = (hTh, a) if a < HALF else (hTo, a - HALF)
        return t[:, :, o:o + n]
    gate_d = nc.dram_tensor("gate_scratch", [2, 128, D], F32, kind="Internal").ap()
    gate_db = Buf("gate_d")
    vnew_d = nc.dram_tensor("vnew_scratch", [128, 3 * 512], BF16, kind="Internal").ap()
    vnew_db = Buf("vnew_d")
    attT_d = nc.dram_tensor("attT_scratch", [512, HALF + 128], BF16, kind="Internal").ap()
    attT_db = Buf("attT_d")
    B.dma(ident_f[:], ident_d, writes=[ident_f.b])
    B.dma(ident_b[:], ident_d, writes=[ident_b.b], queue="pool")

    with contextlib.ExitStack() as ph:
        wada = B.sb(ph, "wada", [128, 8, 3 * D], BF16)
        for kcx in range(8):
            B.dma(wada[:, kcx, :], w_ada[kcx * 128:(kcx + 1) * 128, :], writes=[wada.b], queue="pool")
        bada = B.sb(ph, "bada", [128, 3 * D], F32)
        B.dma(bada[:], b_ada[0:1, :].partition_broadcast(128), writes=[bada.b])
        ngb = B.sb(ph, "ngb", [128, D], F32)
        B.dma(ngb[:], norm_g[0:1, :].partition_broadcast(128), writes=[ngb.b])
        mods = []
        for which, cdram, gidx in (("P", cp, 0), ("S", cs, 1)):
            c32 = B.sb(ph, "c32" + which, [128, D], F32)
            B.dma(c32[:], cdram, writes=[c32.b])
            c16 = B.sb(ph, "c16" + which, [128, D], BF16)
            B.act(c16[:], c32[:], AF.Silu, [c32.b], [c16.b])
            cT = B.sb(ph, "cT" + which, [128, 8, 128], BF16)
            pt = B.ps()
            ptb = pt.t[:].bitcast(BF16)
            for kcx in range(8):
                B.transpose(ptb[:, kcx * 128:(kcx + 1) * 128], c16[:, kcx * 128:(kcx + 1) * 128], ident_b[:],
                            [c16.b, ident_b.b], pt, inc=(kcx == 7))
            B.copy("dve", cT[:].rearrange("p k t -> p (k t)"), ptb[:, 0:1024], [pt.b], [cT.b])
            mod = B.sb(ph, "mod" + which, [128, 3 * D], F32)
            for nb in range(6):
                pt = B.ps()
                B.mm_group(pt.t[:], [(cT[:, kcx, :], wada[:, kcx, nb * 512:(nb + 1) * 512]) for kcx in range(8)],
                           [cT.b, wada.b], pt)
                B.tt("dve", mod[:, nb * 512:(nb + 1) * 512], pt.t[:], bada[:, nb * 512:(nb + 1) * 512], ALU.add,
                     [pt.b, bada.b], [mod.b])
            B.stt("dve", mod[:, D:2 * D], mod[:, D:2 * D], 1.0, ngb[:], ALU.add, ALU.mult, [mod.b, ngb.b], [mod.b])
            B.dma(gate_d[gidx], mod[:, 2 * D:3 * D], reads=[mod.b], writes=[gate_db])
            mods.append(mod)

        issue_cache_shift()
        xbufs = [B.sb(ph, "xin%d" % i, [128, D], F32) for i in range(3)]
        h1s = [B.sb(ph, "h1_%d" % i, [128, D], F32) for i in range(2)]
        h2s = [B.sb(ph, "h2_%d" % i, [128, D], BF16) for i in range(2)]
        junk = B.sb(ph, "junk", [128, D], F32)
        stat = [B.sb(ph, "stat%d" % i, [128, 4], F32) for i in range(2)]
        for ti in range(33):
            xt = xbufs[ti % 3]
            h1 = h1s[ti % 2]
            h2 = h2s[ti % 2]
            sT = stat[ti % 2]
            mod = mods[0] if ti < 32 else mods[1]
            src = xp[ti * 128:(ti + 1) * 128, :] if ti < 32 else xs
            B.dma(xt[:], src, writes=[xt.b])
            B.act(junk[:], xt[:], AF.Square, [xt.b], [junk.b, sT.b], accum_out=sT[:, 0:1])
            B.act(sT[:, 1:2], sT[:, 0:1], AF.Ln, [sT.b], [sT.b], bias=EPS, scale=1.0 / D)
            B.act(sT[:, 2:3], sT[:, 1:2], AF.Exp, [sT.b], [sT.b], scale=-0.5)
            B.stt("dve", h1[:], xt[:], sT[:, 2:3], mod[:, D:2 * D], ALU.mult, ALU.mult, [xt.b, sT.b, mod.b], [h1.b])
            B.tt("dve", h2[:], h1[:], mod[:, 0:D], ALU.add, [h1.b, mod.b], [h2.b])
            pt = B.ps()
            ptb = pt.t[:].bitcast(BF16)
            for kcx in range(8):
                B.transpose(ptb[:, kcx * 128:(kcx + 1) * 128], h2[:, kcx * 128:(kcx + 1) * 128], ident_b[:],
                            [h2.b, ident_b.b], pt, inc=(kcx == 7))
            B.copy("act", Hall(ti * 128, 128), ptb[:, 0:1024].rearrange("p (k t) -> p k t", k=8),
                   [pt.b], [Hb(ti * 128)])

    fw.barrier()
    with contextlib.ExitStack() as ph:
        wkv = [B.sb(ph, "wkv%d" % i, [128, 8, 1024], BF16) for i in range(2)]
        obuf = [B.sb(ph, "kvo%d" % i, [128, 1024], F32) for i in range(3)]
        vtmps = [B.sb(ph, "vtmp%d" % i, [128, 512], BF16) for i in range(2)]
        oi = 0
        for g in range(3):
            w = wkv[g % 2]
            B.dma(w[:, :, 0:512], w_in[:, OFF_K + g * 512:OFF_K + (g + 1) * 512].rearrange("(k p) n -> p k n", p=128),
                  writes=[w.b], queue="pool")
            B.dma(w[:, :, 512:1024], w_in[:, OFF_V + g * 512:OFF_V + (g + 1) * 512].rearrange("(k p) n -> p k n", p=128),
                  writes=[w.b], queue="pool")
            ntile = WINS[g] // 128
            tiles = [(32 - ntile + i, i) for i in range(ntile)] + [(32, None)]
            for (ti, orow) in tiles:
                ob = obuf[oi % 3]
                oi += 1
                for half in range(2):
                    pt = B.ps()
                    B.mm_group(pt.t[:], [(H(kcx, ti * 128, 128), w[:, kcx, half * 512:(half + 1) * 512])
                                         for kcx in range(8)], [Hb(ti * 128), w.b], pt)
                    B.copy("act" if half == 0 else "dve", ob[:, half * 512:(half + 1) * 512], pt.t[:], [pt.b], [ob.b])
                if orow is not None:
                    B.dma(kvp[g][orow * 128:(orow + 1) * 128, :], ob[:], reads=[ob.b])
                else:
                    Lc = WINS[g]
                    for b in range(NB_S):
                        B.dma(kvs[g][b, Lc - 8:Lc, :], ob[b * 8:(b + 1) * 8, :], reads=[ob.b])
                    vtmp = vtmps[g % 2]
                    B.copy("pool", vtmp[:], ob[:, 512:1024], [ob.b], [vtmp.b])
                    B.dma(vnew_d[:, g * 512:(g + 1) * 512], vtmp[:], reads=[vtmp.b], writes=[vnew_db])

        wx = [B.sb(ph, "wx%d" % i, [128, 8, 512], BF16) for i in range(2)]
        cvo = [B.sb(ph, "cvo%d" % i, [128, 3072], F32) for i in range(2)]
        for blk in range(6):
            w = wx[blk % 2]
            B.dma(w[:], w_in[:, OFF_XBC + blk * 512:OFF_XBC + (blk + 1) * 512].rearrange("(k p) n -> p k n", p=128),
                  writes=[w.b], queue="pool")
            for j, ti in enumerate((31, 32)):
                pt = B.ps()
                B.mm_group(pt.t[:], [(H(kcx, ti * 128, 128), w[:, kcx, :]) for kcx in range(8)],
                           [Hb(ti * 128), w.b], pt)
                B.copy("act" if j == 0 else "dve", cvo[j][:, blk * 512:(blk + 1) * 512], pt.t[:], [pt.b], [cvo[j].b])
        B.dma(convp[0:3, :], cvo[0][125:128, :], reads=[cvo[0].b])
        for b in range(NB_S):
            B.dma(convs[b * 3:(b + 1) * 3, :], cvo[1][b * 8 + 5:b * 8 + 8, :], reads=[cvo[1].b])

    ynT_d = nc.dram_tensor("ynT_scratch", [2048, HALF + 128], BF16, kind="Internal").ap()
    ynT_db = Buf("ynT_d")

    def ssd_phase():
        with contextlib.ExitStack() as ph:
            def cload(name, shape, dt_=F32, src=None):
                t = B.sb(ph, name, shape, dt_)
                B.dma(t[:] if len(shape) == 2 else t[:].rearrange("p a b -> p (a b)"), B.ins[src or ("c_" + name)],
                      writes=[t.b], queue=("pool" if dt_ == BF16 else "sp"))
                return t
            triU = cload("tri", [128, 128])
            triS = cload("triS", [128, 128])
            blkS = cload("blk", [128, 128])
            mb4 = cload("mb4", [128, 512], BF16)
            mb4S = cload("mb4S", [128, 512], BF16)
            selA = cload("selA", [8, 1024])
            selB = cload("selB", [8, 2, 512])
            bmask = cload("bmask", [128, 16, 128], BF16)
            rowmask = cload("rowmask", [128, 16])
            onesF = B.sb(ph, "onesF", [128, 128], F32)
            B.memset("pool", onesF[:], 1.0, [onesF.b])
            flg = B.sb(ph, "flg", [128, 2], F32)
            B.dma(flg[:], flag, writes=[flg.b])
            dtb = B.sb(ph, "dtb", [128, 32], F32)
            B.dma(dtb[:], dt_bias[0:1, :].partition_broadcast(128), writes=[dtb.b])
            abc = B.sb(ph, "abc", [128, 32], F32)
            B.dma(abc[:], a_log[0:1, :].partition_broadcast(128), writes=[abc.b])
            B.act(abc[:], abc[:], AF.Exp, [abc.b], [abc.b])
            B.ts("dve", abc[:], abc[:], -1.0, None, ALU.mult, None, [abc.b], [abc.b])
            dsk = B.sb(ph, "dsk", [128, 32], F32)
            B.dma(dsk[:], d_skip[0:1, :].partition_broadcast(128), writes=[dsk.b])
            scvS = B.sb(ph, "scvS", [48, 6, 128], F32)

            wgs = [B.sb(ph, "wg%d" % i, [128, 8, 1288], BF16) for i in range(1)]
            gnb = [B.sb(ph, "gnb%d" % i, [128, 512], F32) for i in range(1)]
            xraw = [B.sb(ph, "xraw%d" % i, [128, 6, 515], BF16) for i in range(1)]
            hist = B.sb(ph, "hist", [128, 6, 3], BF16)
            dg = B.sb(ph, "dg", [128, 6, 4, 128], BF16)
            cwT = B.sb(ph, "cwT", [128, 24, 8], F32)
            class _V2:
                def __init__(self, ap, b):
                    self.ap, self.b = ap, b

                def __getitem__(self, k):
                    return self.ap[k]
            cw5 = B.sb(ph, "cw5", [5, 512], F32)
            for q in range(6):
                B.dma(cw5[0:4, :], conv_w[:, q * 512:(q + 1) * 512], writes=[cw5.b])
                B.dma(cw5[4:5, :], conv_b[:, q * 512:(q + 1) * 512], writes=[cw5.b])
                pt = B.ps()
                for j in range(4):
                    ci = q * 4 + j
                    B.transpose(pt.t[:, j * 8:j * 8 + 5], cw5[:, j * 128:(j + 1) * 128], ident_f[0:5, 0:5],
                                [cw5.b, ident_f.b], pt, inc=(j == 3))
                B.copy("dve", cwT[:, q * 4:(q + 1) * 4, 0:5], pt.t[:, 0:32].rearrange("p (j e) -> p j e", e=8)[:, :, 0:5],
                       [pt.b], [cwT.b])
            ctmp = [B.sb(ph, "ctmp%d" % i, [128, 512], F32) for i in range(2)]
            xcs = [B.sb(ph, "xc%d" % i, [128, 6, 512], BF16) for i in range(2)]
            xtoks = [B.sb(ph, "xtok%d" % i, [128, 4, 512], BF16) for i in range(2)]
            btoks = [B.sb(ph, "btok%d" % i, [128, 4, 128], BF16) for i in range(2)]
            S32 = B.sb(ph, "S32", [128, 512], F32)
            Sbf = [B.sb(ph, "Sbf%d" % i, [128, 512], BF16) for i in range(5)]
            ynTs = [B.sb(ph, "ynT%d" % i, [128, 4, 512], BF16) for i in range(2)]
            R = {}
            for nm, shp, dt_ in (("dts", [128, 8], F32), ("da", [128, 8], F32), ("cs", [128, 8], F32),
                                 ("d2e", [128, 8], F32), ("ecs", [128, 8], F32), ("etot", [128, 8], F32),
                                 ("wl", [128, 8], F32), ("csT", [8, 128], F32), ("ncsT", [8, 128], F32),
                                 ("xdt", [128, 512], BF16), ("xdtw", [128, 512], BF16), ("xsD", [128, 512], BF16),
                                 ("cbm", [128, 128], BF16), ("E", [128, 2, 512], BF16), ("MT", [128, 2, 512], BF16),
                                 ("zs", [128, 512], F32), ("t1", [128, 512], F32),
                                 ("yn", [128, 512], BF16), ("st", [128, 4], F32), ("jk", [128, 512], F32)):
                nb_ = 4 if shp[-1] <= 128 and len(shp) == 2 else 2
                if nm in ("jk", "zs"):
                    nb_ = 1
                tl_ = [B.sb(ph, nm + "%d" % i, shp, dt_) for i in range(nb_)]
                R[nm] = [tl_[i % nb_] for i in range(4)]
            class _View:
                def __init__(self, ap, b):
                    self.ap, self.b = ap, b

                def __getitem__(self, k):
                    return self.ap[k]
            xpadS = _View(xraw[0][:, :, 0:176].rearrange("p c (b w) -> p c b w", w=11), xraw[0].b)
            xcS = _View(xcs[0][:, :, 0:128], xcs[0].b)
            xtokS = _View(xtoks[0][:, 0, :], xtoks[0].b)
            btokS = _View(btoks[0][:, 0, :], btoks[0].b)
            CTm = B.sb(ph, "CTm", [128, 16, 128], BF16)
            inits = [B.sb(ph, "init%d" % i, [128, 4, 128], F32) for i in range(2)]
            initT = [B.sb(ph, "initT%d" % i, [128, 512], BF16) for i in range(1)]
            initT = [initT[0], initT[0]]
            fins = [B.sb(ph, "fin%d" % i, [128, 4, 128], F32) for i in range(2)]
            bmk = [B.sb(ph, "bmk%d" % i, [128, 128], BF16) for i in range(1)]
            bmk = [bmk[0], bmk[0]]
            darep = ctmp[0]
            etc = B.sb(ph, "etc", [128, 4, 16], F32)
            ynTS = _View(ynTs[0][:, :, 0:128], ynTs[0].b)
            rr = {"i": 0, "s": 0, "c": 0}

            def bc8(t):
                return t.unsqueeze(2).to_broadcast([128, 8, 64])

            def v8(ap):
                return ap.rearrange("p (h d) -> p h d", h=8)

            def chunk(g, wg, gn, xs_ap, xs_b, bt_ap, bt_b, BT, CT, xc_b, hcols, own, sample, yn_dst, yn_b):
                i = rr["i"] % 4
                rr["i"] += 1
                r = {k: v[i] for k, v in R.items()}
                tri = triS if sample else triU
                ones_ = blkS if sample else onesF
                mb = mb4S if sample else mb4
                pdt = B.ps()
                B.mm_group(pdt.t[:, 0:8], [(H(k, hcols, 128), wg[:, k, 1280:1288]) for k in range(8)], [Hb(hcols), wg.b], pdt)
                B.tt("dve", r["dts"][:], pdt.t[:, 0:8], dtb[:, g * 8:(g + 1) * 8], ALU.add, [pdt.b, dtb.b], [r["dts"].b])
                B.act(r["dts"][:], r["dts"][:], AF.Exp, [r["dts"].b], [r["dts"].b])
                B.act(r["dts"][:], r["dts"][:], AF.Ln, [r["dts"].b], [r["dts"].b], bias=1.0)
                B.tt("dve", r["da"][:], r["dts"][:], abc[:, g * 8:(g + 1) * 8], ALU.mult, [r["dts"].b, abc.b], [r["da"].b])
                pcs = B.ps()
                B.mm(pcs.t[:, 0:8], tri[:], r["da"][:], True, True, [tri.b, r["da"].b], [pcs.b], True)
                ptot = B.ps()
                B.mm(ptot.t[:, 0:8], ones_[:], r["da"][:], True, True, [ones_.b, r["da"].b], [ptot.b], True)
                B.copy("dve", r["cs"][:], pcs.t[:, 0:8], [pcs.b], [r["cs"].b])
                B.tt("dve", r["d2e"][:], ptot.t[:, 0:8], r["cs"][:], ALU.subtract, [ptot.b, r["cs"].b], [r["d2e"].b])
                B.act(r["d2e"][:], r["d2e"][:], AF.Exp, [r["d2e"].b], [r["d2e"].b])
                B.act(r["etot"][:], ptot.t[:, 0:8], AF.Exp, [ptot.b], [r["etot"].b])
                B.tt("dve", r["wl"][:], r["dts"][:], r["d2e"][:], ALU.mult, [r["dts"].b, r["d2e"].b], [r["wl"].b])
                B.tt("dve", v8(r["xdtw"][:]), v8(xs_ap), bc8(r["wl"][:]), ALU.mult, [xs_b, r["wl"].b], [r["xdtw"].b])
                sprev = Sbf[rr["s"] % 5]
                if not sample:
                    pst = B.ps()
                    B.mm(pst.t[:], bt_ap, r["xdtw"][:], True, True, [bt_b, r["xdtw"].b], [pst.b], True)
                    B.tt("dve", v8(S32[:]), v8(S32[:]), bc8(r["etot"][:]), ALU.mult, [S32.b, r["etot"].b], [S32.b])
                    B.tt("dve", S32[:], S32[:], pst.t[:], ALU.add, [S32.b, pst.b], [S32.b])
                    rr["s"] += 1
                    B.copy("act", Sbf[rr["s"] % 5][:], S32[:], [S32.b], [Sbf[rr["s"] % 5].b])
                if not own:
                    return None

                def body():
                    chunk_body(g, wg, gn, xs_ap, xs_b, bt_ap, bt_b, BT, CT, xc_b, hcols, sample, yn_dst, yn_b, r, sprev, tri, mb)
                return body

            def chunk_body(g, wg, gn, xs_ap, xs_b, bt_ap, bt_b, BT, CT, xc_b, hcols, sample, yn_dst, yn_b, r, sprev, tri, mb):
                B.act(r["ecs"][:], r["cs"][:], AF.Exp, [r["cs"].b], [r["ecs"].b])
                pyo = B.ps()
                B.hold(pyo)
                if not sample:
                    B.mm(pyo.t[:], CT, sprev[:], True, True, [xc_b, sprev.b], [pyo.b], True)
                else:
                    B.tt("dve", CTm[:], CT.unsqueeze(1).to_broadcast([128, 16, 128]), bmask[:], ALU.mult,
                         [xc_b, bmask.b], [CTm.b])
                    B.copy("dve", v8(darep[:]), bc8(r["da"][:]), [r["da"].b], [darep.b])
                    for ci in range(4):
                        pe_ = B.ps()
                        B.mm(pe_.t[:, 0:16], darep[:, ci * 128:(ci + 1) * 128], rowmask[:], True, True,
                             [darep.b, rowmask.b], [pe_.b], True)
                        B.act(etc[:, ci, :], pe_.t[:, 0:16], AF.Exp, [pe_.b], [etc.b])
                    for b in range(NB_S):
                        ini = inits[b % 2]
                        iT = initT[b % 2]
                        B.dma(ini[:], sst[b, g * 512:(g + 1) * 512, :].rearrange("(c p) n -> p c n", p=128), writes=[ini.b])
                        ptr = B.ps()
                        for ci in range(4):
                            B.transpose(ptr.t[:, ci * 128:(ci + 1) * 128], ini[:, ci, :], ident_f[:],
                                        [ini.b, ident_f.b], ptr, inc=(ci == 3))
                        B.copy("act", iT[:], ptr.t[:], [ptr.b], [iT.b])
                        B.mm(pyo.t[:], CTm[:, b, :], iT[:], b == 0, b == NB_S - 1, [CTm.b, iT.b], [pyo.b], b == NB_S - 1)
                        bm = bmk[b % 2]
                        B.ts("dve", bm[:], bt_ap, rowmask[:, b:b + 1], None, ALU.mult, None, [bt_b, rowmask.b], [bm.b])
                        pfs = B.ps()
                        for ci in range(4):
                            B.mm(pfs.t[:, ci * 128:(ci + 1) * 128], r["xdtw"][:, ci * 128:(ci + 1) * 128], bm[:],
                                 ci == 0, ci == 3, [r["xdtw"].b, bm.b], [pfs.b], ci == 3)
                        fn_ = fins[b % 2]
                        for ci in range(4):
                            B.stt("dve", fn_[:, ci, :], ini[:, ci, :], etc[:, ci, b:b + 1], pfs.t[:, ci * 128:(ci + 1) * 128],
                                  ALU.mult, ALU.add, [ini.b, etc.b, pfs.b], [fn_.b])
                        B.dma(ssms[b, g * 512:(g + 1) * 512, :].rearrange("(c p) n -> p c n", p=128), fn_[:], reads=[fn_.b])
                B.copy("act", r["t1"][:], pyo.t[:], [pyo.b], [r["t1"].b])
                B.release(pyo)
                B.tt("dve", v8(r["t1"][:]), v8(r["t1"][:]), bc8(r["ecs"][:]), ALU.mult, [r["t1"].b, r["ecs"].b], [r["t1"].b])
                pT = B.ps()
                B.mm(pT.t[0:8, 0:128], r["da"][:], tri[:], True, True, [r["da"].b, tri.b], [pT.b], True)
                B.copy("dve", r["csT"][:], pT.t[0:8, 0:128], [pT.b], [r["csT"].b])
                B.ts("dve", r["ncsT"][:], pT.t[0:8, 0:128], -1.0, None, ALU.mult, None, [pT.b], [r["ncsT"].b])
                B.tt("dve", v8(r["xdt"][:]), v8(xs_ap), bc8(r["dts"][:]), ALU.mult, [xs_b, r["dts"].b], [r["xdt"].b])
                B.tt("dve", v8(r["xsD"][:]), v8(xs_ap), bc8(dsk[:, g * 8:(g + 1) * 8]), ALU.mult,
                     [xs_b, dsk.b], [r["xsD"].b])
                pcb = B.ps()
                B.mm(pcb.t[:, 0:128], BT, CT, True, True, [xc_b], [pcb.b], True)
                B.tt("dve", r["cbm"][:], pcb.t[:, 0:128], tri[:], ALU.mult, [pcb.b, tri.b], [r["cbm"].b])
                for bk in range(2):
                    pseg = B.ps()
                    B.mm(pseg.t[:], r["ncsT"][:], selB[:, bk, :], True, False, [r["ncsT"].b, selB.b], [pseg.b], False)
                    for j in range(4):
                        h = bk * 4 + j
                        B.mm(pseg.t[:, j * 128:(j + 1) * 128], selA[:, h * 128:(h + 1) * 128], r["csT"][:], False, False,
                             [selA.b, r["csT"].b], [pseg.b], False)
                    B.mm(pseg.t[:], ident_b[:], mb[:], False, True, [ident_b.b, mb.b], [pseg.b], True)
                    B.act(r["E"][:, bk, :], pseg.t[:], AF.Exp, [pseg.b], [r["E"].b])
                    B.tt("dve", r["MT"][:, bk, :].rearrange("p (h l) -> p h l", h=4),
                         r["E"][:, bk, :].rearrange("p (h l) -> p h l", h=4),
                         r["cbm"][:].unsqueeze(1).to_broadcast([128, 4, 128]), ALU.mult,
                         [r["E"].b, r["cbm"].b], [r["MT"].b])
                pyd = B.ps()
                for h in range(8):
                    B.mm(pyd.t[:, h * 64:(h + 1) * 64], r["MT"][:, h // 4, (h % 4) * 128:(h % 4 + 1) * 128],
                         r["xdt"][:, h * 64:(h + 1) * 64], h == 0, False, [r["MT"].b, r["xdt"].b], [pyd.b], False)
                B.mm(pyd.t[:], ident_b[:], r["xsD"][:], False, True, [ident_b.b, r["xsD"].b], [pyd.b], True)
                B.tt("dve", r["t1"][:], r["t1"][:], pyd.t[:], ALU.add, [r["t1"].b, pyd.b], [r["t1"].b])
                pz = B.ps()
                B.mm_group(pz.t[:], [(H(k, hcols, 128), wg[:, k, 768:1280]) for k in range(8)], [Hb(hcols), wg.b], pz)
                B.act(r["zs"][:], pz.t[:], AF.Silu, [pz.b], [r["zs"].b])
                B.tt("dve", r["t1"][:], r["t1"][:], r["zs"][:], ALU.mult, [r["t1"].b, r["zs"].b], [r["t1"].b])
                B.act(r["jk"][:], r["t1"][:], AF.Square, [r["t1"].b], [r["jk"].b, r["st"].b], accum_out=r["st"][:, 0:1])
                B.act(r["st"][:, 1:2], r["st"][:, 0:1], AF.Ln, [r["st"].b], [r["st"].b], bias=EPS, scale=1.0 / 512)
                B.act(r["st"][:, 2:3], r["st"][:, 1:2], AF.Exp, [r["st"].b], [r["st"].b], scale=-0.5)
                B.stt("dve", r["yn"][:], r["t1"][:], r["st"][:, 2:3], gn[:], ALU.mult, ALU.mult,
                      [r["t1"].b, r["st"].b, gn.b], [r["yn"].b])
                pyt = B.ps()
                pytb = pyt.t[:].bitcast(BF16)
                for ci in range(4):
                    B.transpose(pytb[:, ci * 128:(ci + 1) * 128], r["yn"][:, ci * 128:(ci + 1) * 128], ident_b[:],
                                [r["yn"].b, ident_b.b], pyt, inc=(ci == 3))
                B.copy("act", yn_dst, pytb[:, 0:512].rearrange("p (c t) -> p c t", c=4), [pyt.b], [yn_b])

            def conv_silu(g, src, srcb, cidx, nfree, dst, dstb, view):
                tmp = ctmp[rr["c"] % 2]
                rr["c"] += 1
                tv = view(tmp)
                wv = cwT[:, cidx, :]
                B.act(tv, src(3), AF.Identity, [srcb, cwT.b], [tmp.b], bias=wv[:, 4:5], scale=wv[:, 3:4])
                B.stt("dve", tv, src(2), wv[:, 2:3], tv, ALU.mult, ALU.add, [srcb, cwT.b, tmp.b], [tmp.b])
                B.stt("dve", tv, src(1), wv[:, 1:2], tv, ALU.mult, ALU.add, [srcb, cwT.b, tmp.b], [tmp.b])
                B.stt("dve", tv, src(0), wv[:, 0:1], tv, ALU.mult, ALU.add, [srcb, cwT.b, tmp.b], [tmp.b])
                B.act(dst, tv, AF.Silu, [tmp.b], [dstb])

            def chan_idx(g, ci):
                return g * 4 + ci if ci < 4 else (16 + g if ci == 4 else 20 + g)

            def wcol(ci):
                return slice(ci * 128, (ci + 1) * 128)

            for g in range(4):
                wg = wgs[0]
                gn = gnb[0]
                xo = OFF_XBC
                for (c0, n, src0) in ((0, 512, xo + g * 512), (512, 128, xo + 2048 + g * 128), (640, 128, xo + 2560 + g * 128),
                                      (768, 512, OFF_Z + g * 512), (1280, 8, OFF_DT + g * 8)):
                    B.dma(wg[:, :, c0:c0 + n], w_in[:, src0:src0 + n].rearrange("(k p) n -> p k n", p=128),
                          writes=[wg.b], queue="pool")
                B.dma(gn[:], ssm_norm_g[0:1, g * 512:(g + 1) * 512].partition_broadcast(128), writes=[gn.b])
                B.memset("dve", S32[:], 0.0, [S32.b])
                B.memset("pool", Sbf[rr["s"] % 5][:], 0.0, [Sbf[rr["s"] % 5].b])
                for ci in range(6):
                    for w in range(4):
                        B.ts("dve", dg[:, ci, w, :], ident_f[:], cwT[:, chan_idx(g, ci), w:w + 1], None,
                             ALU.mult, None, [ident_f.b, cwT.b], [dg.b])
                for blk in range(8):
                    own = blk >= 4
                    xr = xraw[0]
                    xc = xcs[blk % 2]
                    xt = xtoks[blk % 2]
                    bt = btoks[blk % 2]
                    ynb = ynTs[blk % 2]
                    nch = 6 if own else 5
                    if blk == 0:
                        B.memset("pool", xr[:, :, 0:3], 0.0, [xr.b])
                    elif blk == 4:
                        B.ts("dve", xr[:, :, 0:3], hist[:], flg[:, 0:1], None, ALU.mult, None, [hist.b, flg.b], [xr.b])
                    else:
                        nh = 6 if blk >= 5 else 5
                        B.copy("pool", xr[:, 0:nh, 0:3], hist[:, 0:nh, :], [hist.b], [xr.b])
                    for ci in range(6 if blk >= 3 else 5):
                        pt = B.ps()
                        B.mm_group(pt.t[:], [(wg[:, k, wcol(ci)], H(k, blk * 512, 512)) for k in range(8)],
                                   [wg.b, Hb(blk * 512)], pt)
                        B.copy("act" if ci % 2 == 0 else "dve", xr[:, ci, 3:515], pt.t[:], [pt.b], [xr.b])
                    for ci in range(nch):
                        pcv = B.ps()
                        for w in range(4):
                            B.mm(pcv.t[:], dg[:, ci, w, :], xr[:, ci, w:w + 512], w == 0, w == 3, [dg.b, xr.b], [pcv.b], w == 3)
                        B.act(xc[:, ci, :], pcv.t[:], AF.Silu, [pcv.b, cwT.b], [xc.b], bias=cwT[:, chan_idx(g, ci), 4:5])
                    nh = 6 if blk >= 3 else 5
                    B.copy("pool", hist[:, 0:nh, :], xr[:, 0:nh, 512:515], [xr.b], [hist.b])
                    if blk == 4:
                        B.ts("dve", S32[:], S32[:], flg[:, 0:1], None, ALU.mult, None, [S32.b, flg.b], [S32.b])
                        rr["s"] += 1
                        B.copy("act", Sbf[rr["s"] % 5][:], S32[:], [S32.b], [Sbf[rr["s"] % 5].b])
                    for half in range(2):
                        pt = B.ps()
                        ptb = pt.t[:].bitcast(BF16)
                        for tl in range(2):
                            for ci in range(4):
                                tcol = (half * 2 + tl) * 128
                                B.transpose(ptb[:, tl * 512 + ci * 128: tl * 512 + (ci + 1) * 128], xc[:, ci, tcol:tcol + 128],
                                            ident_b[:], [xc.b, ident_b.b], pt, inc=(tl == 1 and ci == 3))
                        B.copy("act" if half == 0 else "dve", xt[:, half * 2:half * 2 + 2, :],
                               ptb[:, 0:1024].rearrange("p (t c) -> p t c", t=2), [pt.b], [xt.b])
                    pt = B.ps()
                    ptb = pt.t[:].bitcast(BF16)
                    for tl in range(4):
                        B.transpose(ptb[:, tl * 128:(tl + 1) * 128], xc[:, 4, tl * 128:(tl + 1) * 128], ident_b[:],
                                    [xc.b, ident_b.b], pt, inc=(tl == 3))
                    B.copy("dve", bt[:].rearrange("p t n -> p (t n)"), ptb[:, 0:512], [pt.b], [bt.b])
                    bodies = []
                    for tl in range(4):
                        tcol = tl * 128
                        L0 = blk * 512 + tcol
                        bodies.append(chunk(g, wg, gn, xt[:, tl, :], xt.b, bt[:, tl, :], bt.b, xc[:, 4, tcol:tcol + 128],
                                            xc[:, 5, tcol:tcol + 128], xc.b, L0, own, False,
                                            ynb[:, :, tcol:tcol + 128], ynb.b))
                    for bd in bodies:
                        if bd is not None:
                            bd()
                    if own:
                        t0 = (blk - 4) * 512
                        B.dma(ynT_d[g * 512:(g + 1) * 512, t0:t0 + 512].rearrange("(c p) t -> p c t", p=128), ynb[:],
                              reads=[ynb.b], writes=[ynT_db])
                ptr = B.ps()
                for ci in range(4):
                    B.transpose(ptr.t[:, ci * 128:(ci + 1) * 128], S32[:, ci * 128:(ci + 1) * 128], ident_f[:],
                                [S32.b, ident_f.b], ptr, inc=(ci == 3))
                fo = fins[0]
                B.copy("dve", fo[:].rearrange("p c n -> p (c n)"), ptr.t[:], [ptr.b], [fo.b])
                B.dma(ssmp[g * 512:(g + 1) * 512, :].rearrange("(c p) n -> p c n", p=128), fo[:], reads=[fo.b])

                for ci in range(6):
                    cidx = chan_idx(g, ci)
                    B.dma(scvS[:, ci, :], scv[:, cidx * 128:(cidx + 1) * 128], writes=[scvS.b])
                for ci in range(6):
                    cidx = chan_idx(g, ci)
                    pt = B.ps()
                    B.transpose(pt.t[:, 0:48], scvS[:, ci, :], ident_f[0:48, 0:48],
                                [scvS.b, ident_f.b], pt)
                    B.copy("dve", xpadS[:, ci, :, 0:3], pt.t[:, 0:48].rearrange("p (b w) -> p b w", w=3), [pt.b], [xpadS.b])
                    pt = B.ps()
                    B.mm_group(pt.t[:, 0:128], [(wg[:, k, wcol(ci)], H(k, SEQ, 128)) for k in range(8)],
                               [wg.b, Hb(SEQ)], pt)
                    B.copy("act", xpadS[:, ci, :, 3:11], pt.t[:, 0:128].rearrange("p (b s) -> p b s", s=8), [pt.b], [xpadS.b])
                for ci in range(6):
                    conv_silu(g, (lambda k, ci=ci: xpadS[:, ci, :, k:k + 8]), xpadS.b, chan_idx(g, ci), 128,
                              xcS[:, ci, :].rearrange("p (b s) -> p b s", s=8), xcS.b,
                              lambda t: t[:, 0:128].rearrange("p (b s) -> p b s", s=8))
                pt = B.ps()
                ptb = pt.t[:].bitcast(BF16)
                for ci in range(5):
                    B.transpose(ptb[:, ci * 128:(ci + 1) * 128], xcS[:, ci, :], ident_b[:], [xcS.b, ident_b.b], pt, inc=(ci == 4))
                B.copy("act", xtokS[:], ptb[:, 0:512], [pt.b], [xtokS.b])
                B.copy("dve", btokS[:], ptb[:, 512:640], [pt.b], [btokS.b])
                chunk(g, wg, gn, xtokS[:], xtokS.b, btokS[:], btokS.b, xcS[:, 4, :], xcS[:, 5, :], xcS.b,
                      SEQ, True, True, ynTS[:], ynTS.b)()
                B.dma(ynT_d[g * 512:(g + 1) * 512, HALF:HALF + 128].rearrange("(c p) t -> p c t", p=128), ynTS[:],
                      reads=[ynTS.b], writes=[ynT_db])

    fw.barrier()
    if STAGE >= 3:
        ssd_phase()
    fw.barrier()

    def attn_phase():
      with contextlib.ExitStack() as ph0:
        onesb = B.sb(ph0, "onesb", [128, 128], BF16)
        qTSa = B.sb(ph0, "qTSa", [128, 12, 128], BF16)
        qTSb = B.sb(ph0, "qTSb", [128, 12, 128], BF16)
        hm = B.sb(ph0, "hm", [128, 4], F32)
        B.dma(hm[:], B.ins["c_hmask"], writes=[hm.b])
        nn_sb = B.sb(ph0, "nn_sb", [128, 4, 512], F32)
        gattS = B.sb(ph0, "gattS", [128, 4, 128], BF16)
        attS = B.sb(ph0, "attS", [128, 4, 128], BF16)
        with contextlib.ExitStack() as ph:
            attT = B.sb(ph, "attTl", [128, HALF], BF16)
            vnewS = B.sb(ph, "vnewS", [128, 3, 512], BF16)
            B.dma(vnewS[:].rearrange("p g n -> p (g n)"), vnew_d, reads=[vnew_db], writes=[vnewS.b])
            B.memset("pool", onesb[:], 1.0, [onesb.b])
            flg = B.sb(ph, "flgA", [128, 2], F32)
            B.dma(flg[:], flag, writes=[flg.b])
            wsls = [B.sb(ph, "wsl%d" % i, [128, 8, 1280], BF16) for i in range(1)]
            abts = [B.sb(ph, "abt%d" % i, [128, 3, 512], F32) for i in range(1)]
            sbts = [B.sb(ph, "sbt%d" % i, [128, 3, 256], F32) for i in range(1)]
            qTa = B.sb(ph, "qTa", [128, 3, HALF], BF16)
            qTb = B.sb(ph, "qTb", [128, 3, HALF], BF16)
            KOFF = (0, 2176, 4736)
            kT = B.sb(ph, "kT", [128, 8832], BF16)
            Vp = B.sb(ph, "Vp", [128, 69, 128], BF16)
            acc = B.sb(ph, "acc", [128, 2, HALF], F32)
            gatt = B.sb(ph, "gatt", [128, HALF], BF16)
            scs = [B.sb(ph, "sc%d" % i, [128, 512], F32) for i in range(2)]
            PTs = [B.sb(ph, "PT%d" % i, [128, 512], BF16) for i in range(2)]
            kTS = B.sb(ph, "kTS", [128, 12, 128], BF16)
            cnt = {"e": 0, "b": 0}

            def evac(out, in_, reads, writes, scale=None):
                e = "act" if cnt["e"] % 2 == 0 else "dve"
                cnt["e"] += 1
                if scale is None:
                    B.copy(e, out, in_, reads, writes)
                elif e == "act":
                    B.act(out, in_, AF.Copy, reads, writes, scale=scale)
                else:
                    B.ts("dve", out, in_, scale, None, ALU.mult, None, reads, writes)

            def dil_view(ap512, r):
                return ap512.rearrange("p (i r) -> p r i", r=r)

            for j in range(KSLOTS):
                wsl = wsls[0]
                abt = abts[0]
                sbt = sbts[0]
                for g in range(3):
                    for sec, off in enumerate((OFF_Q, OFF_K, OFF_V)):
                        c0 = (sec * 3 + g) * 128
                        s0 = off + g * 512 + j * 128
                        B.dma(wsl[:, :, c0:c0 + 128], w_in[:, s0:s0 + 128].rearrange("(k p) n -> p k n", p=128),
                              writes=[wsl.b], queue="pool")
                B.dma(wsl[:, :, 1152:1280], w_in[:, OFF_GATT + j * 128:OFF_GATT + (j + 1) * 128].rearrange("(k p) n -> p k n", p=128),
                      writes=[wsl.b], queue="pool")
                B.dma(abt[:].rearrange("p g n -> p (g n)"), B.ins["c_abias"][j], writes=[abt.b])
                B.dma(sbt[:].rearrange("p g n -> p (g n)"), B.ins["c_sbias"][j], writes=[sbt.b])
                B.memset("pool", acc[:], 0.0, [acc.b])

                def proj_T(col0, L0, n):
                    pt = B.ps()
                    B.mm_group(pt.t[:, 0:n], [(wsl[:, k, col0:col0 + 128], H(k, L0, n)) for k in range(8)],
                               [wsl.b, Hb(L0)], pt)
                    return pt

                for tb in range(4):
                    L0 = HALF + tb * 512
                    for g in range(3):
                        pt = proj_T(g * 128, L0, 512)
                        for qi_, qTx in enumerate((qTa, qTb)):
                            if g == 0:
                                dst = qTx[:, 0, tb * 512:(tb + 1) * 512]
                                src = pt.t[:]
                            elif g == 1:
                                dst = qTx[:, 1, tb * 512:(tb + 1) * 512].rearrange("p (r i) -> p r i", r=4)
                                src = dil_view(pt.t[:], 4)
                            else:
                                dst = qTx[:, 2, :].rearrange("p (r i) -> p r i", r=16)[:, :, tb * 32:(tb + 1) * 32]
                                src = dil_view(pt.t[:], 16)
                            if qi_ == 0:
                                B.act(dst, src, AF.Identity, [pt.b, hm.b], [qTx.b], scale=hm[:, 0:1])
                            else:
                                B.ts("dve", dst, src, hm[:, 1:2], None, ALU.mult, None, [pt.b, hm.b], [qTx.b])
                        pt = proj_T((3 + g) * 128, L0, 512)
                        if g == 0:
                            dst = kT[:, 128 + tb * 512:128 + (tb + 1) * 512]
                            src = pt.t[:]
                        elif g == 1:
                            o = KOFF[1] + (tb + 1) * 512
                            dst = kT[:, o:o + 512].rearrange("p (r i) -> p r i", r=4)
                            src = dil_view(pt.t[:], 4)
                        else:
                            o = KOFF[2] + 2048
                            dst = kT[:, o:o + 2048].rearrange("p (r i) -> p r i", r=16)[:, :, tb * 32:(tb + 1) * 32]
                            src = dil_view(pt.t[:], 16)
                        evac(dst, src, [pt.b], [kT.b])
                    pt = proj_T(1152, L0, 512)
                    B.act(gatt[:, tb * 512:(tb + 1) * 512], pt.t[:], AF.Silu, [pt.b], [gatt.b])
                pt = proj_T(3 * 128, HALF - 128, 128)
                evac(kT[:, 0:128], pt.t[:, 0:128], [pt.b], [kT.b])
                pt = proj_T(4 * 128, HALF - 512, 512)
                evac(kT[:, KOFF[1]:KOFF[1] + 512].rearrange("p (r i) -> p r i", r=4), dil_view(pt.t[:], 4), [pt.b], [kT.b])
                for tb in range(4):
                    pt = proj_T(5 * 128, tb * 512, 512)
                    evac(kT[:, KOFF[2]:KOFF[2] + 2048].rearrange("p (r i) -> p r i", r=16)[:, :, tb * 32:(tb + 1) * 32],
                         dil_view(pt.t[:], 16), [pt.b], [kT.b])
                for g in range(3):
                    pt = proj_T(g * 128, SEQ, 128)
                    B.act(qTSa[:, g * 4 + j, :], pt.t[:, 0:128], AF.Identity, [pt.b, hm.b], [qTSa.b], scale=hm[:, 0:1])
                    B.ts("dve", qTSb[:, g * 4 + j, :], pt.t[:, 0:128], hm[:, 1:2], None, ALU.mult, None, [pt.b, hm.b], [qTSb.b])
                    pt = proj_T((3 + g) * 128, SEQ, 128)
                    evac(kTS[:, g * 4 + j, :], pt.t[:, 0:128], [pt.b], [kTS.b])
                pt = proj_T(1152, SEQ, 128)
                B.act(gattS[:, j, :], pt.t[:, 0:128], AF.Silu, [pt.b], [gattS.b])
                vblocks = []
                for kb in range(17):
                    vblocks.append((0, HALF - 128 + kb * 128, 1))
                for b1 in range(-1, 4):
                    for r4 in range(4):
                        vblocks.append((1, HALF + b1 * 512 + r4, 4))
                for hb in range(2):
                    for r16 in range(16):
                        vblocks.append((2, hb * HALF + r16, 16))
                for q0 in range(0, 69, 4):
                    grp = vblocks[q0:q0 + 4]
                    pt = B.ps()
                    for bi, (g, La, step) in enumerate(grp):
                        last = (bi == len(grp) - 1)
                        for k in range(8):
                            B.mm(pt.t[:, bi * 128:(bi + 1) * 128], H(k, La, 128, step), wsl[:, k, (6 + g) * 128:(7 + g) * 128],
                                 (bi == 0 and k == 0), (last and k == 7), [Hb(La), wsl.b], [pt.b], (last and k == 7))
                    n = len(grp)
                    evac(Vp[:, q0:q0 + n, :], pt.t[:, 0:n * 128].rearrange("p (b c) -> p b c", c=128), [pt.b], [Vp.b])

                for g in range(3 if KSUB >= 2 else 0):
                    for qb in range(16):
                        if g == 0:
                            kown = KOFF[0] + (qb + 1) * 128
                            kprev = kown - 128
                            vown, vprev = 1 + qb, qb
                            halo = (qb == 0)
                            tok = lambda t, qb=qb: t[:, :, qb * 128:(qb + 1) * 128]
                        elif g == 1:
                            b1, r4 = qb // 4, qb % 4
                            kown = KOFF[1] + (b1 + 1) * 512 + r4 * 128
                            kprev = kown - 512
                            vown = 17 + (b1 + 1) * 4 + r4
                            vprev = vown - 4
                            halo = (b1 == 0)
                            tok = lambda t, b1=b1, r4=r4: t[:, :, b1 * 512 + r4:b1 * 512 + r4 + 509:4]
                        else:
                            kown = KOFF[2] + 2048 + qb * 128
                            kprev = kown - 2048
                            vown = 37 + 16 + qb
                            vprev = vown - 16
                            halo = True
                            tok = lambda t, qb=qb: t[:, :, qb:qb + 2033:16]
                        i = cnt["b"] % 2
                        cnt["b"] += 1
                        sc, PT = scs[i], PTs[i]
                        psc = B.ps()
                        n = 0
                        for kbi, kcol in enumerate((kprev, kown)):
                            for h in range(2):
                                qTx = qTa if h == 0 else qTb
                                B.mm(psc.t[:, (kbi * 2 + h) * 128:(kbi * 2 + h + 1) * 128], kT[:, kcol:kcol + 128],
                                     qTx[:, g, qb * 128:(qb + 1) * 128], n == 0, n == 3, [kT.b, qTx.b], [psc.b], n == 3)
                                n += 1
                        if halo:
                            B.stt("dve", sc[:, 0:256], psc.t[:, 0:256], flg[:, 1:2], abt[:, g, 0:256], ALU.add, ALU.add,
                                  [psc.b, flg.b, abt.b], [sc.b])
                            B.tt("dve", sc[:, 256:512], psc.t[:, 256:512], abt[:, g, 256:512], ALU.add, [psc.b, abt.b], [sc.b])
                        else:
                            B.tt("dve", sc[:], psc.t[:], abt[:, g, :], ALU.add, [psc.b, abt.b], [sc.b])
                        B.act(PT[:], sc[:], AF.Exp, [sc.b], [PT.b])
                        if KBLK < 2:
                            continue
                        ppv = B.ps()
                        for kbi, vb in enumerate((vprev, vown)):
                            B.mm(ppv.t[:, 0:256], Vp[:, vb, :], PT[:, kbi * 256:(kbi + 1) * 256], kbi == 0, kbi == 1,
                                 [Vp.b, PT.b], [ppv.b], False)
                        for kbi in range(2):
                            B.mm(ppv.t[:, 256:512], onesb[:], PT[:, kbi * 256:(kbi + 1) * 256], False, kbi == 1,
                                 [onesb.b, PT.b], [ppv.b], kbi == 1)
                        for h in range(2 if KBLK >= 3 else 0):
                            src = ppv.t[:].rearrange("p (nd hh q) -> p nd hh q", nd=2, hh=2)[:, :, h, :]
                            dst = tok(acc[:])
                            B.stt("dve", dst, src, hm[:, 2 + h:3 + h], dst, ALU.mult, ALU.add, [acc.b, ppv.b, hm.b], [acc.b])
                B.recip(acc[:, 1, :], acc[:, 1, :], [acc.b], [acc.b])
                B.tt("dve", acc[:, 0, :], acc[:, 0, :], acc[:, 1, :], ALU.mult, [acc.b], [acc.b])
                B.tt("dve", attT[:], acc[:, 0, :], gatt[:, 0:HALF], ALU.mult, [acc.b, gatt.b], [attT.b])
                B.dma(attT_d[j * 128:(j + 1) * 128, 0:HALF], attT[:], reads=[attT.b], writes=[attT_db])

                ppn = B.ps()
                B.hold(ppn)
                for g in range(3 if KSUB >= 3 else 0):
                    gc = g * 4 + j
                    i = cnt["b"] % 2
                    cnt["b"] += 1
                    sc, PT = scs[i], PTs[i]
                    psn = B.ps()
                    for h in range(2):
                        qTx = qTSa if h == 0 else qTSb
                        B.mm(psn.t[:, h * 128:(h + 1) * 128], kTS[:, gc, :], qTx[:, gc, :], h == 0, h == 1,
                             [kTS.b, qTx.b], [psn.b], h == 1)
                    B.tt("dve", sc[:, 0:256], psn.t[:, 0:256], sbt[:, g, :], ALU.add, [psn.b, sbt.b], [sc.b])
                    B.act(PT[:, 0:256], sc[:, 0:256], AF.Exp, [sc.b], [PT.b])
                    B.mm(ppn.t[:, 0:256], vnewS[:, g, j * 128:(j + 1) * 128], PT[:, 0:256], g == 0, g == 2,
                         [vnewS.b, PT.b], [ppn.b], False)
                    B.mm(ppn.t[:, 256:512], onesb[:], PT[:, 0:256], False, g == 2, [onesb.b, PT.b], [ppn.b], g == 2)
                B.copy("act", nn_sb[:, j, :], ppn.t[:], [ppn.b], [nn_sb.b])
                B.release(ppn)

        fw.barrier()
        with contextlib.ExitStack() as ph:
            cnt = {"e": 0}

            def evac(out, in_, reads, writes):
                e = "act" if cnt["e"] % 2 == 0 else "dve"
                cnt["e"] += 1
                B.copy(e, out, in_, reads, writes)
            cbias = B.sb(ph, "cbias", [128, 832], F32)
            B.dma(cbias[:], B.ins["c_cbias"], writes=[cbias.b])
            QBD = B.sb(ph, "QBD", [128, 12, 16, 16], BF16)
            B.copy("dve", QBD[:, :, :, 0:8], qTSa[:].rearrange("p g (b s) -> p g b s", s=8), [qTSa.b], [QBD.b])
            B.copy("dve", QBD[:, :, :, 8:16], qTSb[:].rearrange("p g (b s) -> p g b s", s=8), [qTSb.b], [QBD.b])
            kvbs = [B.sb(ph, "kvb%d" % i, [128, 13, 1024], BF16) for i in range(2)]
            KTs = [B.sb(ph, "KTb%d" % i, [128, 4, 13, 128], BF16) for i in range(2)]
            scS = [B.sb(ph, "scS%d" % i, [128, 832], F32) for i in range(2)]
            PTS = [B.sb(ph, "PTS%d" % i, [128, 832], BF16) for i in range(2)]
            nd_sb = B.sb(ph, "nd_sb", [128, 16, 128], F32)
            for b in range(NB_S if KSUB >= 4 else 0):
                kvb, KT, sc, PT = kvbs[b % 2], KTs[b % 2], scS[b % 2], PTS[b % 2]
                B.dma(kvb[:, 0, :], kc[0][b], writes=[kvb.b], queue="pool")
                B.dma(kvb[:, 1:5, :], kc[1][b].rearrange("(p r) n -> p r n", r=4), writes=[kvb.b], queue="pool")
                B.dma(kvb[:, 5:13, :], kc[2][b].rearrange("(p r) n -> p r n", r=16)[:, 0:8, :], writes=[kvb.b], queue="pool")
                for c in range(4):
                    for (t0, nt) in ((0, 8), (8, 5)):
                        pt = B.ps()
                        ptb = pt.t[:].bitcast(BF16)
                        for ti in range(nt):
                            B.transpose(ptb[:, ti * 128:(ti + 1) * 128], kvb[:, t0 + ti, c * 128:(c + 1) * 128], ident_b[:],
                                        [kvb.b, ident_b.b], pt, inc=(ti == nt - 1))
                        evac(KT[:, c, t0:t0 + nt, :], ptb[:, 0:nt * 128].rearrange("p (t q) -> p t q", q=128), [pt.b], [KT.b])
                for (t0, nt) in ((0, 8), (8, 5)):
                    pss = B.ps()
                    n = 0
                    for ti in range(nt):
                        t = t0 + ti
                        g = 0 if t == 0 else (1 if t < 5 else 2)
                        for c in range(4):
                            B.mm(pss.t[:, ti * 64 + c * 16:ti * 64 + (c + 1) * 16], KT[:, c, t, :], QBD[:, g * 4 + c, b, :],
                                 n == 0, n == nt * 4 - 1, [KT.b, QBD.b], [pss.b], n == nt * 4 - 1)
                            n += 1
                    B.tt("dve", sc[:, t0 * 64:(t0 + nt) * 64], pss.t[:, 0:nt * 64], cbias[:, t0 * 64:(t0 + nt) * 64], ALU.add,
                         [pss.b, cbias.b], [sc.b])
                B.act(PT[:], sc[:], AF.Exp, [sc.b], [PT.b])
                ppv = B.ps()
                n = 0
                for c in range(4):
                    for t in range(13):
                        B.mm(ppv.t[:, c * 16:(c + 1) * 16], kvb[:, t, 512 + c * 128:512 + (c + 1) * 128],
                             PT[:, t * 64 + c * 16:t * 64 + (c + 1) * 16], n == 0, False, [kvb.b, PT.b], [ppv.b], False)
                        n += 1
                for t in range(13):
                    B.mm(ppv.t[:, 64:128], onesb[:], PT[:, t * 64:(t + 1) * 64], False, t == 12, [onesb.b, PT.b], [ppv.b], t == 12)
                B.copy("act", nd_sb[:, b, :], ppv.t[:, 0:128], [ppv.b], [nd_sb.b])
            numS = B.sb(ph, "numS", [128, 128], F32)
            denS = B.sb(ph, "denS", [128, 128], F32)
            tmpA = B.sb(ph, "tmpA", [128, 128], F32)
            tmpB = B.sb(ph, "tmpB", [128, 128], F32)
            v = lambda ap: ap.rearrange("p (b s) -> p b s", s=8)
            for c in range(4):
                for (dstT, ndoff, nnoff) in ((numS, c * 16, 0), (denS, 64 + 2 * c * 8, 256)):
                    B.tt("dve", v(tmpA[:]), nd_sb[:, :, ndoff:ndoff + 8], v(nn_sb[:, c, nnoff:nnoff + 128]), ALU.add,
                         [nd_sb.b, nn_sb.b], [tmpA.b])
                    B.tt("dve", v(tmpB[:]), nd_sb[:, :, ndoff + 8:ndoff + 16], v(nn_sb[:, c, nnoff + 128:nnoff + 256]), ALU.add,
                         [nd_sb.b, nn_sb.b], [tmpB.b])
                    B.ts("dve", dstT[:], tmpA[:], hm[:, 2:3], None, ALU.mult, None, [tmpA.b, hm.b], [dstT.b])
                    B.stt("dve", dstT[:], tmpB[:], hm[:, 3:4], dstT[:], ALU.mult, ALU.add, [tmpB.b, hm.b, dstT.b], [dstT.b])
                B.recip(denS[:], denS[:], [denS.b], [denS.b])
                B.tt("dve", numS[:], numS[:], denS[:], ALU.mult, [numS.b, denS.b], [numS.b])
                B.tt("dve", attS[:, c, :], numS[:], gattS[:, c, :], ALU.mult, [numS.b, gattS.b], [attS.b])
            B.dma(attT_d[:, HALF:HALF + 128].rearrange("(c p) t -> p c t", p=128), attS[:], reads=[attS.b], writes=[attT_db])

    if STAGE >= 4:
        attn_phase()
    fw.barrier()

    halo_scope.close()

    def back_phase():
        with contextlib.ExitStack() as ph:
            watt = B.sb(ph, "watt", [128, 4, D], BF16)
            B.dma(watt[:], w_att.rearrange("(k p) n -> p k n", p=128), writes=[watt.b], queue="pool")
            wssm = B.sb(ph, "wssm", [128, 16, D], BF16)
            for q in range(4):
                B.dma(wssm[:, q * 4:(q + 1) * 4, :], w_ssm[q * 512:(q + 1) * 512, :].rearrange("(k p) n -> p k n", p=128),
                      writes=[wssm.b], queue="pool")
            wout = B.sb(ph, "wout", [128, 8, D], BF16)
            for q in range(2):
                B.dma(wout[:, q * 4:(q + 1) * 4, :], w_out[q * 512:(q + 1) * 512, :].rearrange("(k p) n -> p k n", p=128),
                      writes=[wout.b], queue="pool")
            gateP = B.sb(ph, "gateP", [128, D], F32)
            gateS = B.sb(ph, "gateS", [128, D], F32)
            B.dma(gateP[:], gate_d[0], reads=[gate_db], writes=[gateP.b])
            B.dma(gateS[:], gate_d[1], reads=[gate_db], writes=[gateS.b])
            attb = B.sb(ph, "attb", [128, 4, 512], BF16)
            fng = B.sb(ph, "fng", [128, D], F32)
            B.dma(fng[:], fnorm_g[0:1, :].partition_broadcast(128), writes=[fng.b])
            wgab = B.sb(ph, "wgab", [128, 8, 2048], BF16)
            for q in range(4):
                B.dma(wgab[:, :, q * 512:(q + 1) * 512], w_in[:, OFF_GA + q * 512:OFF_GA + (q + 1) * 512].rearrange("(k p) n -> p k n", p=128),
                      writes=[wgab.b], queue="pool")
            ynb = B.sb(ph, "ynblk", [128, 16, 512], BF16)
            sga = [B.sb(ph, "sga%d" % i, [128, 512], F32) for i in range(2)]
            sgb = [B.sb(ph, "sgb%d" % i, [128, 512], F32) for i in range(2)]
            ta = [B.sb(ph, "ta%d" % i, [128, 512], F32) for i in range(2)]
            tb_ = [B.sb(ph, "tb%d" % i, [128, 512], F32) for i in range(2)]
            mg = B.sb(ph, "mg", [128, 8, 512], BF16)
            xin = [B.sb(ph, "xres%d" % i, [128, D], F32) for i in range(2)]
            ob = [B.sb(ph, "ob%d" % i, [128, D], F32) for i in range(2)]
            jk = B.sb(ph, "jkb", [128, D], F32)
            stt_ = [B.sb(ph, "stb%d" % i, [128, 4], F32) for i in range(2)]
            it = 0
            for blk in range(5):
                n = 512 if blk < 4 else 128
                t0 = blk * 512
                for q in range(4):
                    B.dma(ynb[:, q * 4:(q + 1) * 4, 0:n], ynT_d[q * 512:(q + 1) * 512, t0:t0 + n].rearrange("(c p) t -> p c t", p=128),
                          reads=[ynT_db], writes=[ynb.b])
                B.dma(attb[:, :, 0:n], attT_d[:, t0:t0 + n].rearrange("(c p) t -> p c t", p=128), reads=[attT_db], writes=[attb.b])
                for dc in range(8):
                    w = wgab
                    i = dc % 2
                    pga = B.ps()
                    B.mm_group(pga.t[:, 0:n], [(w[:, k, dc * 128:(dc + 1) * 128], hTo[:, k, t0:t0 + n]) for k in range(8)], [w.b, hTo.b], pga)
                    B.act(sga[i][:, 0:n], pga.t[:, 0:n], AF.Sigmoid, [pga.b], [sga[i].b])
                    pgb = B.ps()
                    B.mm_group(pgb.t[:, 0:n], [(w[:, k, 1024 + dc * 128:1024 + (dc + 1) * 128], hTo[:, k, t0:t0 + n]) for k in range(8)], [w.b, hTo.b], pgb)
                    B.act(sgb[i][:, 0:n], pgb.t[:, 0:n], AF.Sigmoid, [pgb.b], [sgb[i].b])
                    pa = B.ps()
                    B.mm_group(pa.t[:, 0:n], [(watt[:, k, dc * 128:(dc + 1) * 128], attb[:, k, 0:n]) for k in range(4)],
                               [watt.b, attb.b], pa)
                    B.tt("dve", ta[i][:, 0:n], pa.t[:, 0:n], sga[i][:, 0:n], ALU.mult, [pa.b, sga[i].b], [ta[i].b])
                    pm = B.ps()
                    B.mm_group(pm.t[:, 0:n], [(wssm[:, k, dc * 128:(dc + 1) * 128], ynb[:, k, 0:n]) for k in range(16)],
                               [wssm.b, ynb.b], pm)
                    B.tt("dve", tb_[i][:, 0:n], pm.t[:, 0:n], sgb[i][:, 0:n], ALU.mult, [pm.b, sgb[i].b], [tb_[i].b])
                    B.tt("dve", mg[:, dc, 0:n], ta[i][:, 0:n], tb_[i][:, 0:n], ALU.add, [ta[i].b, tb_[i].b], [mg.b])
                for tl in range(n // 128):
                    xi = xin[it % 2]
                    o = ob[it % 2]
                    sT = stt_[it % 2]
                    it += 1
                    if blk < 4:
                        row = blk * 512 + tl * 128
                        B.dma(xi[:], xp[HALF + row:HALF + row + 128, :], writes=[xi.b])
                        gate, dst = gateP, yp[row:row + 128, :]
                    else:
                        B.dma(xi[:], xs, writes=[xi.b])
                        gate, dst = gateS, ys
                    for half in range(2):
                        po = B.ps()
                        B.mm_group(po.t[:], [(mg[:, k, tl * 128:(tl + 1) * 128], wout[:, k, half * 512:(half + 1) * 512])
                                             for k in range(8)], [mg.b, wout.b], po)
                        B.tt("dve", o[:, half * 512:(half + 1) * 512], po.t[:], gate[:, half * 512:(half + 1) * 512], ALU.mult,
                             [po.b, gate.b], [o.b])
                    B.tt("dve", o[:], o[:], xi[:], ALU.add, [o.b, xi.b], [o.b])
                    B.act(jk[:], o[:], AF.Square, [o.b], [jk.b, sT.b], accum_out=sT[:, 0:1])
                    B.act(sT[:, 1:2], sT[:, 0:1], AF.Ln, [sT.b], [sT.b], bias=EPS, scale=1.0 / D)
                    B.act(sT[:, 2:3], sT[:, 1:2], AF.Exp, [sT.b], [sT.b], scale=-0.5)
                    B.stt("dve", o[:], o[:], sT[:, 2:3], fng[:], ALU.mult, ALU.mult, [o.b, sT.b, fng.b], [o.b])
                    B.dma(dst, o[:], reads=[o.b])

    fw.barrier()
    if STAGE >= 5:
        back_phase()
    else:
        with contextlib.ExitStack() as ph:
            z = B.sb(ph, "zeros", [128, 1024], F32)
            B.memset("dve", z[:], 0.0, [z.b])
            for i in range(16):
                B.dma(yp[i * 128:(i + 1) * 128, :], z[:], reads=[z.b])
                if STAGE < 3:
                    B.dma(ssmp[i * 128:(i + 1) * 128, :], z[:, 0:128], reads=[z.b])
            B.dma(ys, z[:], reads=[z.b])
            if STAGE < 3:
                for b in range(NB_S):
                    for hh in range(2):
                        B.dma(ssms[b, hh * 1024:(hh + 1) * 1024].rearrange("(a p) n -> p a n", p=128),
                              z[:].rearrange("p (a n) -> p a n", a=8), reads=[z.b])

    fw.finish()
    fw.emit()
    root.close()
    return B


def _host_consts():
    NEGM = -30000.0
    idx = np.arange(128)
    s_, l_ = idx[:, None], idx[None, :]
    tri = (s_ <= l_).astype(np.float32)
    same = ((s_ // 8) == (l_ // 8)).astype(np.float32)
    triS = tri * same
    mb = np.where(l_ >= s_, 0.0, NEGM).astype(np.float32)
    mbS = np.where((l_ >= s_) & (same > 0), 0.0, NEGM).astype(np.float32)
    selA = np.zeros((8, 8, 128), np.float32)
    selB = np.zeros((8, 2, 4, 128), np.float32)
    for h in range(8):
        selA[h, h, :] = 1.0
        selB[h, h // 4, h % 4, :] = 1.0
    bmask = np.zeros((128, 16, 128), np.float32)
    for b in range(16):
        bmask[:, b, b * 8:(b + 1) * 8] = 1.0
    rowmask = np.zeros((128, 16), np.float32)
    rowmask[idx, idx // 8] = 1.0
    n = 24
    slopes = (2.0 ** (-8.0 * np.arange(1, n + 1) / n)).reshape(3, 8)
    abias = np.full((4, 128, 3, 2, 2, 128), NEGM, np.float64)
    kj, qi = idx[:, None], idx[None, :]
    for g in range(3):
        dil = DILS[g]
        for kbi in range(2):
            delta = qi + 128 - kj if kbi == 0 else qi - kj
            valid = (delta >= 0) & (delta <= 128)
            for j in range(4):
                for h in range(2):
                    sl = slopes[g, 2 * j + h]
                    abias[j, :, g, kbi, h, :] = np.where(valid, -sl * delta * dil, NEGM)
    sbias = np.full((4, 128, 3, 2, 128), NEGM, np.float64)
    kb_, ks_ = idx[:, None] // 8, idx[:, None] % 8
    qb_, qs_ = idx[None, :] // 8, idx[None, :] % 8
    for g in range(3):
        dil = DILS[g]
        d = qs_ - ks_
        valid = (kb_ == qb_) & (d >= 0) & (d % dil == 0)
        for j in range(4):
            for h in range(2):
                sl = slopes[g, 2 * j + h]
                sbias[j, :, g, h, :] = np.where(valid, -sl * d, NEGM)
    cbias = np.full((128, 13, 8, 8), NEGM, np.float64)
    p_ = idx
    for h in range(8):
        for s in range(8):
            row = p_
            cbias[:, 0, h, s] = np.where(row >= s, -slopes[0, h] * (128 + s - row), NEGM)
            r = s % 4
            row = r + 4 * p_
            cbias[:, 1 + r, h, s] = np.where(row >= s, -slopes[1, h] * (512 + s - row), NEGM)
            row = s + 16 * p_
            cbias[:, 5 + s, h, s] = -slopes[2, h] * (2048 + s - row)
    f32 = lambda a: np.ascontiguousarray(a, dtype=np.float32)
    return {
        "c_tri": f32(tri), "c_triS": f32(triS), "c_blk": f32(same), "c_mb4": f32(np.tile(mb, (1, 4))),
        "c_mb4S": f32(np.tile(mbS, (1, 4))), "c_selA": f32(selA.reshape(8, 1024)), "c_selB": f32(selB.reshape(8, 1024)),
        "c_bmask": f32(bmask.reshape(128, 2048)), "c_rowmask": f32(rowmask),
        "c_abias": f32(abias.reshape(4, 128, 1536)), "c_sbias": f32(sbias.reshape(4, 128, 768)),
        "c_cbias": f32(cbias.reshape(128, 832)),
        "c_hmask": f32(np.stack([np.where(idx < 64, 0.125, 0.0), np.where(idx >= 64, 0.125, 0.0),
                                 np.where(idx < 64, 1.0, 0.0), np.where(idx >= 64, 1.0, 0.0)], axis=1)),
    }


_PROG = None


def _get_prog():
    global _PROG
    if _PROG is None:
        _PROG = build_program()
    return _PROG


def kernel(x_prompt, x_sample, c_prompt, c_sample, cache_kv_w128, cache_kv_w512, cache_kv_w2048,
           state_ssm, state_conv, norm_g, w_ada, b_ada, w_in, conv_w, conv_b, dt_bias, a_log,
           d_skip, ssm_norm_g, w_att_branch, w_ssm_branch, w_out, final_norm_g):
    f = lambda a: np.ascontiguousarray(np.asarray(a, dtype=np.float32))
    B = _get_prog()
    caches = (f(cache_kv_w128), f(cache_kv_w512), f(cache_kv_w2048))
    x_prompt = f(x_prompt)
    shared = {
        "w_ada": f(w_ada)[0], "b_ada": f(b_ada)[0][None, :], "w_in": f(w_in)[0], "conv_w": f(conv_w)[0],
        "conv_b": f(conv_b)[0][None, :], "dt_bias": f(dt_bias)[0][None, :], "a_log": f(a_log)[0][None, :],
        "d_skip": f(d_skip)[0][None, :], "ssm_norm_g": f(ssm_norm_g)[0][None, :], "w_att": f(w_att_branch)[0],
        "w_ssm": f(w_ssm_branch)[0], "w_out": f(w_out)[0], "norm_g": f(norm_g)[0][None, :],
        "fnorm_g": f(final_norm_g)[None, :], "ident": np.eye(128, dtype=np.float32),
    }
    shared.update(_host_consts())
    in_maps = []
    for c in range(8):
        b, hf = c // 2, c % 2
        if hf == 1:
            xpc = x_prompt[b]
        else:
            xpc = np.concatenate([np.zeros((HALF, D), np.float32), x_prompt[b, :HALF]], axis=0)
        sl = slice(c * NB_S, (c + 1) * NB_S)
        flagc = np.zeros((128, 2), np.float32)
        flagc[:, 0] = float(hf)
        flagc[:, 1] = 0.0 if hf == 1 else -30000.0
        m = dict(shared)
        m.update({
            "xp": np.ascontiguousarray(xpc),
            "xs": np.ascontiguousarray(f(x_sample)[sl].reshape(128, D)),
            "cp": np.ascontiguousarray(np.broadcast_to(f(c_prompt)[b][None, :], (128, D))),
            "cs": np.ascontiguousarray(np.repeat(f(c_sample)[sl], 8, axis=0)),
            "flag": flagc,
            "sst": np.ascontiguousarray(f(state_ssm)[0, sl].reshape(NB_S, 2048, 128)),
            "scv": np.ascontiguousarray(f(state_conv)[0, sl].reshape(NB_S * 3, 3072)),
        })
        for g in range(3):
            m["kc%d" % g] = np.ascontiguousarray(caches[g][0, sl].reshape(NB_S, WINS[g], 1024))
        m = {k: v for k, v in m.items() if k in B.ins}
        in_maps.append(m)
    res = run_bass_kernel_spmd(B.nc, in_maps, core_ids=list(range(8)))
    R = res.results
    y_prompt = np.stack([np.concatenate([R[2 * b]["yp"], R[2 * b + 1]["yp"]], axis=0) for b in range(4)])
    y_sample = np.concatenate([R[c]["ys"].reshape(NB_S, 8, D) for c in range(8)], axis=0)
    kvp = [np.stack([R[2 * b + 1]["kvp%d" % g].reshape(WINS[g], 2, 8, 64) for b in range(4)])[None] for g in range(3)]
    ssm_p = np.stack([R[2 * b + 1]["ssmp"].reshape(32, 64, 128) for b in range(4)])[None]
    conv_p = np.stack([R[2 * b + 1]["convp"] for b in range(4)])[None]
    kvs = [np.concatenate([R[c]["kvs%d" % g].reshape(NB_S, WINS[g], 2, 8, 64) for c in range(8)], axis=0)[None]
           for g in range(3)]
    ssm_s = np.concatenate([R[c]["ssms"].reshape(NB_S, 32, 64, 128) for c in range(8)], axis=0)[None]
    conv_s = np.concatenate([R[c]["convs"].reshape(NB_S, 3, 3072) for c in range(8)], axis=0)[None]
    return (y_prompt, y_sample, kvp[0], kvp[1], kvp[2], ssm_p, conv_p, kvs[0], kvs[1], kvs[2], ssm_s, conv_s)
```
